# Optimizing a Trainium2 kernel written in Bass

```python
import jax, jax.numpy as jnp
from jax import lax
import numpy as np

D_MODEL = 1024
BATCH = 8
SEQ = 2048
DEPTH = 2

N_MIXERS = 2
N_MLA_LAYERS = (DEPTH + 1) // 2
N_SB_LAYERS = DEPTH // 2
BLOCK_Q = 128
EPS = 1e-6
MLA_HEADS = 8
MLA_Q_LORA = 384
MLA_KV_LORA = 256
MLA_NOPE = 128
MLA_ROPE = 64
MLA_V = 128
ROPE_THETA = 10000.0
POS_OFFSET_MAX = 4096
SB_HEADS = 8
SB_HEAD_DIM = D_MODEL // SB_HEADS
MEM_LEN = 256
MEM_HEADS = 4
MEM_HEAD_DIM = 128
D_FF = 4 * D_MODEL

kernel_name = "hybrid_mla_stickbreaking_memxattn_sqrelu"

NEG_BIG = -1e30


def rmsnorm(x, g):
    xf = x.astype(jnp.float32)
    y = xf * lax.rsqrt(jnp.mean(xf * xf, axis=-1, keepdims=True) + EPS)
    return (y * g.astype(jnp.float32)).astype(x.dtype)


def rope_tables(positions):
    inv_freq = ROPE_THETA ** (-jnp.arange(0, MLA_ROPE, 2, dtype=jnp.float32) / MLA_ROPE)
    ang = positions.astype(jnp.float32)[..., None] * inv_freq
    return jnp.cos(ang)[:, :, None, :], jnp.sin(ang)[:, :, None, :]


def apply_rope(x, cos, sin):
    half = x.shape[-1] // 2
    x1, x2 = x[..., :half], x[..., half:]
    c = cos.astype(x.dtype)
    s = sin.astype(x.dtype)
    return jnp.concatenate([x1 * c - x2 * s, x1 * s + x2 * c], axis=-1)


def blocked_causal_softmax_attn(q, k, v, scale):
    S = q.shape[1]
    outs = []
    for i in range(S // BLOCK_Q):
        lo, hi = i * BLOCK_Q, (i + 1) * BLOCK_Q
        s = jnp.einsum('bqhd,bkhd->bhqk', q[:, lo:hi], k[:, :hi]).astype(jnp.float32) * scale
        causal = (lo + jnp.arange(BLOCK_Q))[:, None] >= jnp.arange(hi)[None, :]
        p = jax.nn.softmax(jnp.where(causal, s, NEG_BIG), axis=-1).astype(v.dtype)
        outs.append(jnp.einsum('bhqk,bkhd->bqhd', p, v[:, :hi]))
    return jnp.concatenate(outs, axis=1)


def mla_mixer(h, cos, sin, w_dkv, g_q, g_kv, w_uq, w_ukv, w_o):
    B, S, _ = h.shape
    lat = h @ w_dkv
    c_q = rmsnorm(lat[..., :MLA_Q_LORA], g_q)
    c_kv = rmsnorm(lat[..., MLA_Q_LORA:MLA_Q_LORA + MLA_KV_LORA], g_kv)
    k_pe = lat[..., MLA_Q_LORA + MLA_KV_LORA:][:, :, None, :]
    q = (c_q @ w_uq).reshape(B, S, MLA_HEADS, MLA_NOPE + MLA_ROPE)
    kv = (c_kv @ w_ukv).reshape(B, S, MLA_HEADS, MLA_NOPE + MLA_V)
    q_pe = apply_rope(q[..., MLA_NOPE:], cos, sin)
    k_pe = apply_rope(k_pe, cos, sin)
    q = jnp.concatenate([q[..., :MLA_NOPE], q_pe], axis=-1)
    k = jnp.concatenate([kv[..., :MLA_NOPE],
                         jnp.broadcast_to(k_pe, (B, S, MLA_HEADS, MLA_ROPE))], axis=-1)
    v = kv[..., MLA_NOPE:]
    o = blocked_causal_softmax_attn(q, k, v, (MLA_NOPE + MLA_ROPE) ** -0.5)
    return o.reshape(B, S, MLA_HEADS * MLA_V) @ w_o


def stick_breaking_mixer(h, w_qkv, w_o):
    B, S, _ = h.shape
    qkv = (h @ w_qkv).reshape(B, S, 3, SB_HEADS, SB_HEAD_DIM)
    q, k, v = qkv[:, :, 0], qkv[:, :, 1], qkv[:, :, 2]
    scale = SB_HEAD_DIM ** -0.5
    outs = []
    for i in range(S // BLOCK_Q):
        lo, hi = i * BLOCK_Q, (i + 1) * BLOCK_Q
        z = jnp.einsum('bqhd,bkhd->bhqk', q[:, lo:hi], k[:, :hi]).astype(jnp.float32) * scale
        strict = jnp.arange(hi)[None, :] < (lo + jnp.arange(BLOCK_Q))[:, None]
        log_fail = jnp.where(strict, -jax.nn.softplus(z), 0.0)
        excl = lax.cumsum(log_fail, axis=3, reverse=True) - log_fail
        a = jnp.where(strict, jnp.exp(jax.nn.log_sigmoid(z) + excl), 0.0)
        outs.append(jnp.einsum('bhqk,bkhd->bqhd', a.astype(v.dtype), v[:, :hi]))
    o = jnp.concatenate(outs, axis=1)
    return o.reshape(B, S, SB_HEADS * SB_HEAD_DIM) @ w_o


def mem_cross_attn(h, m, w_q, w_kv, w_o):
    B, S, _ = h.shape
    q = (h @ w_q).reshape(B, S, MEM_HEADS, MEM_HEAD_DIM)
    kv = (m @ w_kv).reshape(B, m.shape[1], 2, MEM_HEADS, MEM_HEAD_DIM)
    s = jnp.einsum('bqhd,bmhd->bhqm', q, kv[:, :, 0]).astype(jnp.float32) * MEM_HEAD_DIM ** -0.5
    p = jax.nn.softmax(s, axis=-1).astype(h.dtype)
    o = jnp.einsum('bhqm,bmhd->bqhd', p, kv[:, :, 1])
    return o.reshape(B, S, MEM_HEADS * MEM_HEAD_DIM) @ w_o


def sq_relu_mlp(h, w_in, w_out):
    return jnp.square(jax.nn.relu(h @ w_in)) @ w_out


def setup_inputs(seed: int = 0) -> dict:
    key = jax.random.key(seed)
    ks = iter(jax.random.split(key, 32))
    f32 = jnp.float32

    def w(shape, fan_in):
        return jax.random.normal(next(ks), shape, f32) * (fan_in ** -0.5)

    def gain(shape):
        return 1.0 + 0.02 * jax.random.normal(next(ks), shape, f32)

    x = jax.random.normal(next(ks), (BATCH, SEQ, D_MODEL), f32)
    mem = jax.random.normal(next(ks), (BATCH, MEM_LEN, D_MODEL), f32)
    offset = jax.random.randint(next(ks), (BATCH, 1), 0, POS_OFFSET_MAX, dtype=jnp.int32)
    positions = (offset + jnp.arange(SEQ, dtype=jnp.int32)[None, :]).astype(jnp.int32)
    return {
        "x": x,
        "mem": mem,
        "positions": positions,
        "norm_mix": gain((DEPTH, D_MODEL)),
        "norm_cross": gain((DEPTH, D_MODEL)),
        "norm_mem": gain((DEPTH, D_MODEL)),
        "norm_mlp": gain((DEPTH, D_MODEL)),
        "norm_final": gain((D_MODEL,)),
        "mla_w_dkv": w((N_MLA_LAYERS, D_MODEL, MLA_Q_LORA + MLA_KV_LORA + MLA_ROPE), D_MODEL),
        "mla_g_q": gain((N_MLA_LAYERS, MLA_Q_LORA)),
        "mla_g_kv": gain((N_MLA_LAYERS, MLA_KV_LORA)),
        "mla_w_uq": w((N_MLA_LAYERS, MLA_Q_LORA, MLA_HEADS * (MLA_NOPE + MLA_ROPE)), MLA_Q_LORA),
        "mla_w_ukv": w((N_MLA_LAYERS, MLA_KV_LORA, MLA_HEADS * (MLA_NOPE + MLA_V)), MLA_KV_LORA),
        "mla_w_o": w((N_MLA_LAYERS, MLA_HEADS * MLA_V, D_MODEL), MLA_HEADS * MLA_V),
        "sb_w_qkv": w((N_SB_LAYERS, D_MODEL, 3 * SB_HEADS * SB_HEAD_DIM), D_MODEL),
        "sb_w_o": w((N_SB_LAYERS, SB_HEADS * SB_HEAD_DIM, D_MODEL), SB_HEADS * SB_HEAD_DIM),
        "xa_w_q": w((DEPTH, D_MODEL, MEM_HEADS * MEM_HEAD_DIM), D_MODEL),
        "xa_w_kv": w((DEPTH, D_MODEL, 2 * MEM_HEADS * MEM_HEAD_DIM), D_MODEL),
        "xa_w_o": w((DEPTH, MEM_HEADS * MEM_HEAD_DIM, D_MODEL), MEM_HEADS * MEM_HEAD_DIM),
        "mlp_w_in": w((DEPTH, D_MODEL, D_FF), D_MODEL),
        "mlp_w_out": w((DEPTH, D_FF, D_MODEL), D_FF),
    }


def reference(x, mem, positions, norm_mix, norm_cross, norm_mem, norm_mlp, norm_final,
              mla_w_dkv, mla_g_q, mla_g_kv, mla_w_uq, mla_w_ukv, mla_w_o,
              sb_w_qkv, sb_w_o, xa_w_q, xa_w_kv, xa_w_o, mlp_w_in, mlp_w_out):
    cos, sin = rope_tables(positions)
    h = x
    for i in range(DEPTH):
        a = rmsnorm(h, norm_mix[i])
        j = i // N_MIXERS
        if i % N_MIXERS == 0:
            h = h + mla_mixer(a, cos, sin, mla_w_dkv[j], mla_g_q[j], mla_g_kv[j],
                              mla_w_uq[j], mla_w_ukv[j], mla_w_o[j])
        else:
            h = h + stick_breaking_mixer(a, sb_w_qkv[j], sb_w_o[j])
        h = h + mem_cross_attn(rmsnorm(h, norm_cross[i]), rmsnorm(mem, norm_mem[i]),
                               xa_w_q[i], xa_w_kv[i], xa_w_o[i])
        h = h + sq_relu_mlp(rmsnorm(h, norm_mlp[i]), mlp_w_in[i], mlp_w_out[i])
    return rmsnorm(h, norm_final)
```

```python
import numpy as np
import concourse.bass as bass
import concourse.mybir as mybir

F32 = mybir.dt.float32
BF16 = mybir.dt.bfloat16
I32 = mybir.dt.int32
AF = mybir.ActivationFunctionType
ALU = mybir.AluOpType
AX = mybir.AxisListType
DSZ = {F32: 4, BF16: 2, I32: 4}
BLK = 256
ATTACH_WAIT = 1


def ap_range(ap):
    sp = str(ap.space)
    if 'SB' in sp.upper() or 'STATE' in sp.upper():
        space = 'sb'
    elif 'PSUM' in sp.upper():
        space = 'ps'
    else:
        return None
    dsz = DSZ[ap.dtype]
    pat = ap.ap
    pstep = pat[0][0]
    off = ap.offset % pstep if pstep > 0 else ap.offset
    ext = 1
    for st, cnt in pat[1:]:
        ext += (cnt - 1) * abs(st)
    lo, hi = off * dsz, (off + ext) * dsz
    if space == 'ps':
        lo = lo // 2048 * 2048
        hi = (hi + 2047) // 2048 * 2048
    return space, lo, hi


class Op:
    __slots__ = ('eng', 'fn', 'deps', 'needs_inc', 'inc_val', 'is_dma', 'dsem', 'dval', 'idx', 'name', 'seq', 'Kc')

    def __init__(self, eng, fn, is_dma=False, name=''):
        self.eng = eng
        self.fn = fn
        self.deps = []
        self.needs_inc = False
        self.inc_val = None
        self.is_dma = is_dma
        self.dsem = None
        self.dval = None
        self.name = name


class Tracker:
    def __init__(self, nbytes):
        n = (nbytes + BLK - 1) // BLK
        self.w = [None] * n
        self.r = [None] * n

    def access(self, op, lo, hi, write):
        deps = []
        for b in range(lo // BLK, (hi - 1) // BLK + 1):
            w = self.w[b]
            if write:
                if w is not None:
                    deps.append((w, 'WAW'))
                rs = self.r[b]
                if rs:
                    for r in rs.values():
                        deps.append((r, 'WAR'))
                self.w[b] = op
                self.r[b] = None
            else:
                if w is not None:
                    deps.append((w, 'RAW'))
                rs = self.r[b]
                if rs is None:
                    rs = self.r[b] = {}
                key = id(op) if op.is_dma else op.eng
                rs[key] = op
        return deps


class Builder:
    ENGS = ['pe', 'act', 'dve', 'pool', 'sp']

    def __init__(self, nc, n_dma_sems=6):
        self.nc = nc
        self.ops = {e: [] for e in self.ENGS}
        self.trk = {'sb': Tracker(nc.SBUF_PARTITION_SIZE_BYTES), 'ps': Tracker(16384)}
        self.n_dma_sems = n_dma_sems
        self.dma_count = {}
        self.dma_hist = {}
        self.out_dmas = []
        self.nseq = 0

    def add(self, eng, fn, reads=(), writes=(), is_dma=False, name=''):
        op = Op(eng, fn, is_dma, name)
        seen = set()
        for aps, wr in ((reads, False), (writes, True)):
            for ap in aps:
                if ap is None or isinstance(ap, (int, float)):
                    continue
                r = ap_range(ap)
                if r is None:
                    continue
                for (p, kind) in self.trk[r[0]].access(op, r[1], r[2], wr or r[0] == 'ps'):
                    if p is op:
                        continue
                    if (not p.is_dma) and (not op.is_dma) and p.eng == eng:
                        if eng == 'pe':
                            continue
                    if id(p) in seen:
                        continue
                    seen.add(id(p))
                    op.deps.append(p)
        if is_dma:
            q = eng
            n = self.dma_count.get(q, 0)
            k = n % self.n_dma_sems
            prev = self.dma_hist.get((q, k))
            if prev is not None and id(prev) not in seen:
                op.deps.append(prev)
            self.dma_hist[(q, k)] = op
            self.dma_count[q] = n + 1
            op.dsem = (q, k)
            op.dval = 16 * (n // self.n_dma_sems + 1)
        op.seq = self.nseq
        self.nseq += 1
        op.idx = len(self.ops[eng])
        self.ops[eng].append(op)
        return op

    def mm(self, out, lhsT, rhs, start=True, stop=True):
        return self.add('pe', lambda e: e.matmul(out, lhsT, rhs, start=start, stop=stop),
                        reads=[lhsT, rhs], writes=[out])

    def tr(self, out, in_, ident):
        return self.add('pe', lambda e: e.transpose(out, in_, ident), reads=[in_, ident], writes=[out])

    def act(self, out, in_, func, bias=0.0, scale=1.0):
        rd = [in_]
        if not isinstance(bias, (int, float)):
            rd.append(bias)
        if not isinstance(scale, (int, float)):
            rd.append(scale)
        return self.add('act', lambda e: e.activation(out, in_, func, bias=bias, scale=scale),
                        reads=rd, writes=[out])

    def tt(self, eng, out, in0, in1, op):
        return self.add(eng, lambda e: e.tensor_tensor(out, in0, in1, op), reads=[in0, in1], writes=[out])

    def ts(self, eng, out, in0, s1, s2, op0, op1=None):
        rd = [in0]
        for s in (s1, s2):
            if s is not None and not isinstance(s, (int, float)):
                rd.append(s)
        if op1 is None:
            return self.add(eng, lambda e: e.tensor_scalar(out, in0, s1, None, op0), reads=rd, writes=[out])
        return self.add(eng, lambda e: e.tensor_scalar(out, in0, s1, s2, op0, op1), reads=rd, writes=[out])

    def stt(self, eng, out, in0, scalar, in1, op0, op1):
        rd = [in0, in1]
        if not isinstance(scalar, (int, float)):
            rd.append(scalar)
        return self.add(eng, lambda e: e.scalar_tensor_tensor(out, in0, scalar, in1, op0, op1),
                        reads=rd, writes=[out])

    def copy(self, eng, out, in_):
        if eng == 'act':
            return self.act(out, in_, AF.Copy)
        return self.add(eng, lambda e: e.tensor_copy(out, in_), reads=[in_], writes=[out])

    def memset(self, eng, out, val):
        return self.add(eng, lambda e: e.memset(out, val), writes=[out])

    def dma(self, q, out, in_, is_output=False):
        op = self.add(q, lambda e: e.dma_start(out=out, in_=in_), reads=[in_], writes=[out], is_dma=True)
        if is_output:
            self.out_dmas.append(op)
        return op

    def emit(self):
        nc = self.nc
        fin = Op('sp', None)
        fin.deps = list(self.out_dmas)
        fin.seq = self.nseq
        fin.idx = len(self.ops['sp'])
        self.ops['sp'].append(fin)
        allops = sorted((op for e in self.ENGS for op in self.ops[e]), key=lambda o: o.seq)
        lastK = {e: {} for e in self.ENGS}
        n_before = n_after = 0
        for op in allops:
            K = dict(lastK[op.eng])
            surv = []
            for p in sorted(op.deps, key=lambda q: -q.seq):
                n_before += 1
                if p.is_dma:
                    if K.get(('d',) + p.dsem, -1) >= p.dval:
                        continue
                else:
                    if K.get(p.eng, -1) >= p.idx:
                        continue
                surv.append(p)
                for k, v in p.Kc.items():
                    if K.get(k, -1) < v:
                        K[k] = v
            n_after += len(surv)
            op.deps = surv
            lastK[op.eng] = K
            kc = dict(K)
            if op.is_dma:
                kc[('d',) + op.dsem] = op.dval
            else:
                kc[op.eng] = max(kc.get(op.eng, -1), op.idx)
            op.Kc = kc
        self.prune_stats = (n_before, n_after)
        for e in self.ENGS:
            for op in self.ops[e]:
                for p in op.deps:
                    if not p.is_dma:
                        p.needs_inc = True
        cnt = {}
        for e in self.ENGS:
            c = 0
            for op in self.ops[e]:
                if op.needs_inc and not op.is_dma:
                    c += 1
                    op.inc_val = c
            cnt[e] = c
        self.sem_counts = cnt
        sems = {e: nc.alloc_semaphore(name=f"s_{e}") for e in self.ENGS}
        dsems = {}
        for (q, k) in self.dma_hist.keys():
            dsems[(q, k)] = nc.alloc_semaphore(name=f"d_{q}{k}")
        engobj = {'pe': 'tensor', 'act': 'scalar', 'dve': 'vector', 'pool': 'gpsimd', 'sp': 'sync'}

        def run(e, eng):
            waited = {}
            for op in self.ops[e]:
                need = {}
                for p in op.deps:
                    if p.is_dma:
                        key, val, sem = ('d',) + p.dsem, p.dval, dsems[p.dsem]
                    else:
                        key, val, sem = ('c', p.eng), p.inc_val, sems[p.eng]
                    if waited.get(key, 0) >= val:
                        continue
                    if key not in need or need[key][1] < val:
                        need[key] = (sem, val)
                items = list(need.items())
                attach = []
                while items and op.fn is not None and len(attach) < ATTACH_WAIT:
                    attach.append(items.pop())
                for key, (sem, val) in items:
                    waited[key] = val
                    eng.wait_ge(sem, val)
                if op.fn is None:
                    continue
                ins = op.fn(eng)
                for key, (sem, val) in attach:
                    waited[key] = val
                    ins._wait_ge(sem, val)
                if op.is_dma:
                    ins.then_inc(dsems[op.dsem], 16)
                elif op.needs_inc:
                    ins.then_inc(sems[e], 1)

        with nc.Block() as block:
            @block.tensor
            def _(eng):
                run('pe', eng)

            @block.scalar
            def _(eng):
                run('act', eng)

            @block.vector
            def _(eng):
                run('dve', eng)

            @block.gpsimd
            def _(eng):
                run('pool', eng)

            @block.sync
            def _(eng):
                run('sp', eng)
        return {e: len(self.ops[e]) for e in self.ENGS}, cnt


class Arena:
    def __init__(self, nc, nbytes):
        self.nc = nc
        self.n = nbytes
        self.t = nc.alloc_sbuf_tensor("arena", [128, nbytes // 4], F32)
        self.top = 0
        self.peak = 0

    def mark(self):
        return self.top

    def release(self, m):
        self.top = m

    def alloc(self, shape_free, dtype, parts=128):
        n = int(np.prod(shape_free)) * DSZ[dtype]
        n4 = (n + 255) // 256 * 256
        lo = self.top
        self.top += n4
        self.peak = max(self.peak, self.top)
        assert self.top <= self.n, f"arena overflow {self.top} > {self.n}"
        self.last_lo = lo
        return self.view(lo, shape_free, dtype, parts)

    def view(self, lo, shape_free, dtype, parts=128):
        n = int(np.prod(shape_free)) * DSZ[dtype]
        n4 = (n + 255) // 256 * 256
        v = self.t[0:parts, lo // 4:(lo + n4) // 4]
        if dtype != F32:
            v = v.bitcast(dtype)
        v = v[:, 0:int(np.prod(shape_free))]
        if len(shape_free) == 2:
            v = v.rearrange("p (a b) -> p a b", a=shape_free[0])
        elif len(shape_free) == 3:
            v = v.rearrange("p (a b c) -> p a b c", a=shape_free[0], b=shape_free[1])
        return v


from concourse.bass_utils import run_bass_kernel_spmd

S = 2048
D = 1024
SC_MLA = 192 ** -0.5
SC_SB = 128 ** -0.5
SC_XA = 128 ** -0.5
EPS = 1e-6
NEGV = -30000.0
G_MIX = [0, 32]
G_CROSS = [8, 40]
G_MEM = [16, 48]
G_MLP = [24, 56]
G_FINAL = 64
G_Q = 72
G_KV = 75
G_INVF = 77
NG = 80
C1 = 6.28125
C2 = float(2 * np.pi - 6.28125)
PI_LO = 3.1415925


class Rot:
    def __init__(self, items):
        self.items = list(items)
        self.i = 0

    def next(self):
        v = self.items[self.i % len(self.items)]
        self.i += 1
        return v


def build_program(stage=99):
    nc = bass.Bass("TRN2", target_bir_lowering=False)

    def din(name, shape, dt=F32):
        return nc.dram_tensor(name, shape, dt, kind="ExternalInput").ap()

    x = din("x", [S, D])
    mem = din("mem", [256, D])
    pos = din("pos", [1, S], I32)
    gv = din("gv", [128, NG])
    cst = din("cst", [128, 7 * 128])
    w_dkv = din("w_dkv", [1024, 704])
    w_uq = din("w_uq", [384, 1536])
    w_ukv = din("w_ukv", [256, 2048])
    w_mo = din("w_mo", [1024, 1024])
    w_qkv = din("w_qkv", [1024, 3072])
    w_so = din("w_so", [1024, 1024])
    xq = [din(f"xq{i}", [1024, 512]) for i in range(2)]
    xkv = [din(f"xkv{i}", [1024, 1024]) for i in range(2)]
    xo = [din(f"xo{i}", [512, 1024]) for i in range(2)]
    wi = [din(f"wi{i}", [1024, 4096]) for i in range(2)]
    wo = [din(f"wo{i}", [4096, 1024]) for i in range(2)]
    if stage >= 99:
        out = nc.dram_tensor("out", [S, D], F32, kind="ExternalOutput").ap()
    else:
        dbg = nc.dram_tensor("dbg", [128, 8 * S], F32, kind="ExternalOutput").ap()

    A = Arena(nc, 212480)
    PS = nc.alloc_psum_tensor("ps", [128, 4096], F32)
    bank = [PS[:, i * 512:(i + 1) * 512] for i in range(8)]
    b = Builder(nc)

    def wload(dst, src):
        b.dma('pool', dst, src.rearrange("(c p) n -> p c n", p=128))

    HT = A.alloc([8, S], F32)
    AN = A.alloc([8, S], BF16)
    IDF = A.alloc([128], F32)
    CB = A.alloc([6, 128], BF16)
    GV = A.alloc([NG], F32)
    EPSC = A.alloc([1], F32)
    SQ = [A.alloc([512], BF16) for _ in range(3)]
    RSTD = [A.alloc([512], F32) for _ in range(2)]
    IDB = CB[:, 0, :]
    ONES = CB[:, 1, :]
    NEGONES = CB[:, 2, :]
    NEGTRI = CB[:, 3, :]
    NMS = CB[:, 4, :]
    NMI = CB[:, 5, :]
    b.dma('sp', IDF, cst[:, 0:128])
    b.dma('sp', GV, gv)
    b.dma('pool', CB, cst[:, 128:896].rearrange("p (a c) -> p a c", a=6))
    b.memset('dve', EPSC, EPS)
    cnt = {'sq': 0, 'rs': 0}
    fin_state = {}
    pending = []
    mlp_pre = {}

    def rmsnorm(src, C, gcol, dst, T, nfeat, rot, sq_eng='pool'):
        for t0 in range(0, T, 512):
            n = min(512, T - t0)
            ss = rot.next()
            for c in range(C):
                sq = SQ[cnt['sq'] % 3]
                cnt['sq'] += 1
                if sq_eng == 'act' or (sq_eng == 'mix' and c % 2 == 0):
                    b.act(sq[:, :n], src[:, c, t0:t0 + n], AF.Square)
                else:
                    b.tt('pool', sq[:, :n], src[:, c, t0:t0 + n], src[:, c, t0:t0 + n], ALU.mult)
                b.mm(ss[:, :n], ONES, sq[:, :n], start=(c == 0), stop=(c == C - 1))
            rs = RSTD[cnt['rs'] % 2]
            cnt['rs'] += 1
            b.act(rs[:, :n], ss[:, :n], AF.Ln, bias=EPSC, scale=1.0 / nfeat)
            b.act(rs[:, :n], rs[:, :n], AF.Exp, scale=-0.5)
            for c in range(C):
                b.stt('dve', dst[:, c, t0:t0 + n], src[:, c, t0:t0 + n], GV[:, gcol + c:gcol + c + 1],
                      rs[:, :n], ALU.mult, ALU.mult)

    def actmul(out_, in_, c):
        b.add('act', lambda e: e.mul(out_, in_, c), reads=[in_], writes=[out_])

    def out_proj(W, KC, OTt, rot, tail):
        for tc in range(4):
            ts_ = slice(tc * 512, (tc + 1) * 512)
            for n in range(8):
                bk = rot.next()
                for hc in range(KC):
                    b.mm(bk, W[:, hc, n * 128:(n + 1) * 128], OTt[:, hc, ts_], start=(hc == 0), stop=(hc == KC - 1))
                b.tt('dve', HT[:, n, ts_], bk, HT[:, n, ts_], ALU.add)
            if tc >= 1:
                tail(tc - 1)
        pending.append(lambda: tail(3))

    def flush_pending():
        while pending:
            pending.pop(0)()

    def norm_tail(gcol):
        def f(tc):
            ts_ = slice(tc * 512, (tc + 1) * 512)
            rmsnorm(HT[:, :, ts_], 8, gcol, AN[:, :, ts_], 512, 1024, rotAll, sq_eng='mix')
        return f

    rotAll = Rot(bank)

    m_mla = A.mark()
    COS = A.alloc([S], F32, parts=64)
    SINS = A.alloc([S], F32, parts=64)
    m0 = A.mark()
    XS = [A.alloc([1024], F32) for _ in range(6)]
    for tb in range(6):
        b.dma('sp', XS[tb], x[tb * 128:(tb + 1) * 128, :])
    m2 = A.mark()
    HS = S // 2
    TIs = [A.alloc([HS], I32, parts=64) for _ in range(2)]
    TA = A.alloc([HS], F32, parts=64)
    TK = A.alloc([HS], F32, parts=64)
    TR = A.alloc([HS], F32, parts=64)
    for hf in range(2):
        b.dma('sp', TIs[hf], pos[:, hf * HS:(hf + 1) * HS].partition_broadcast(64))
    for hf in range(2):
        hs_ = slice(hf * HS, (hf + 1) * HS)
        TI = TIs[hf]
        b.copy('dve', TA, TI)
        b.ts('dve', TA, TA, GV[0:64, G_INVF:G_INVF + 1], None, ALU.mult)
        b.ts('dve', TK, TA, float(1.0 / (2 * np.pi)), None, ALU.mult)
        b.copy('dve', TI, TK)
        b.copy('dve', TK, TI)
        b.stt('dve', TR, TK, -C1, TA, ALU.mult, ALU.add)
        b.stt('dve', TR, TK, -C2, TR, ALU.mult, ALU.add)
        b.ts('dve', SINS[:, hs_], TR, PI_LO, -PI_LO, ALU.min, ALU.max)
        b.ts('dve', TA, TR, float(np.pi / 2), None, ALU.add)
        b.ts('dve', TK, TA, float(np.pi), float(-2 * np.pi), ALU.is_gt, ALU.mult)
        b.tt('dve', TA, TA, TK, ALU.add)
        b.ts('dve', COS[:, hs_], TA, PI_LO, -PI_LO, ALU.min, ALU.max)
    A.release(m2)

    def rope_sin_tables():
        b.act(SINS[0:32, :], SINS[0:32, :], AF.Sin, scale=-1.0)
        b.act(SINS[32:64, :], SINS[32:64, :], AF.Sin)
        b.act(COS, COS, AF.Sin)

    for tb in range(16):
        xs = XS[tb % 6]
        for half in range(2):
            bk = rotAll.next()
            for j in range(4):
                c = half * 4 + j
                b.tr(bk[:, j * 128:(j + 1) * 128], xs[:, c * 128:(c + 1) * 128], IDF)
            b.copy('act', HT[:, half * 4:(half + 1) * 4, tb * 128:(tb + 1) * 128],
                   bk.rearrange("p (c t) -> p c t", c=4))
        if tb + 6 < 16:
            b.dma('sp', XS[tb % 6], x[(tb + 6) * 128:(tb + 7) * 128, :])
        if tb % 4 == 3 and tb >= 7 and stage >= 1:
            tcn = tb // 4 - 1
            tsn = slice(tcn * 512, (tcn + 1) * 512)
            rmsnorm(HT[:, :, tsn], 8, G_MIX[0], AN[:, :, tsn], 512, 1024, rotAll, sq_eng='act')
    A.release(m0)
    rope_sin_tables()

    def mla_phase(tail, COS, SINS, m0):
        CQ = A.alloc([3, S], BF16)
        CKV = A.alloc([2, S], BF16)
        KPE = A.alloc([S], BF16, parts=64)
        WMO = A.alloc([8, 1024], BF16)
        WUQ = [A.alloc([3, 256], BF16) for _ in range(2)]
        WUKV = [A.alloc([2, 256], BF16) for _ in range(2)]
        RT1 = A.alloc([512], F32, parts=64)
        RT2 = A.alloc([512], F32, parts=64)
        m1 = A.mark()
        WDKV = A.alloc([8, 768], BF16)
        wload(WDKV[:, :, 0:704], w_dkv)
        wload(WDKV[:, :, 704:736], w_dkv[:, 672:704])
        wload(WDKV[:, :, 736:768], w_dkv[:, 640:672])

        def loadh(h):
            sl = h % 2
            wload(WUQ[sl][:, :, 0:192], w_uq[:, h * 192:(h + 1) * 192])
            wload(WUQ[sl][:, :, 192:224], w_uq[:, h * 192 + 160:h * 192 + 192])
            wload(WUQ[sl][:, :, 224:256], w_uq[:, h * 192 + 128:h * 192 + 160])
            wload(WUKV[sl], w_ukv[:, h * 256:(h + 1) * 256])
        loadh(0)
        loadh(1)
        wload(WMO, w_mo)
        def rope(psA, psB, ts_, dst, scale):
            b.stt('dve', RT1, psA, float(scale), COS[:, ts_], ALU.mult, ALU.mult)
            b.stt('dve', RT2, psB, float(scale), SINS[:, ts_], ALU.mult, ALU.mult)
            b.tt('pool', dst, RT1, RT2, ALU.add)

        rmsnorm(HT[:, :, 1536:2048], 8, G_MIX[0], AN[:, :, 1536:2048], 512, 1024, rotAll, sq_eng='mix')
        LAT = [A.alloc([5, 512], F32) for _ in range(2)]
        def lat_norms(tc):
            ts_ = slice(tc * 512, (tc + 1) * 512)
            lat = LAT[tc % 2]
            rmsnorm(lat[:, 0:3, :], 3, G_Q, CQ[:, :, ts_], 512, 384, rotAll, sq_eng='mix')
            rmsnorm(lat[:, 3:5, :], 2, G_KV, CKV[:, :, ts_], 512, 256, rotAll, sq_eng='mix')

        for tc in range(4):
            ts_ = slice(tc * 512, (tc + 1) * 512)
            lat = LAT[tc % 2]
            for m in range(5):
                bk = rotAll.next()
                for kc in range(8):
                    b.mm(bk, WDKV[:, kc, m * 128:(m + 1) * 128], AN[:, kc, ts_], start=(kc == 0), stop=(kc == 7))
                b.copy('act', lat[:, m, :], bk)
            bk1 = rotAll.next()
            bk2 = rotAll.next()
            for kc in range(8):
                b.mm(bk1[0:64, :], WDKV[:, kc, 640:704], AN[:, kc, ts_], start=(kc == 0), stop=(kc == 7))
            for kc in range(8):
                b.mm(bk2[0:64, :], WDKV[:, kc, 704:768], AN[:, kc, ts_], start=(kc == 0), stop=(kc == 7))
            rope(bk1[0:64, :], bk2[0:64, :], ts_, KPE[:, ts_], 1.0)
            if tc >= 1:
                lat_norms(tc - 1)
        lat_norms(3)
        A.release(m1)
        OT = AN
        QN = A.alloc([S], BF16)
        QR = A.alloc([S], BF16, parts=64)
        KN = A.alloc([S], BF16)
        V = A.alloc([16, 128], BF16)
        PT = [A.alloc([512], BF16) for _ in range(4)]
        RDEN = [A.alloc([512], F32) for _ in range(2)]
        rotZ = Rot([bank[0], bank[1], bank[2]])
        rotO = Rot([bank[3], bank[4], bank[5], bank[6]])
        rotA = Rot([bank[7], bank[0], bank[1], bank[2]])
        npt = 0
        for h in range(8):
            sl = h % 2
            for tc in range(4):
                ts_ = slice(tc * 512, (tc + 1) * 512)
                bk = rotA.next()
                for kc in range(3):
                    b.mm(bk, WUQ[sl][:, kc, 0:128], CQ[:, kc, ts_], start=(kc == 0), stop=(kc == 2))
                actmul(QN[:, ts_], bk, SC_MLA)
                bk1 = rotA.next()
                bk2 = rotA.next()
                for kc in range(3):
                    b.mm(bk1[0:64, :], WUQ[sl][:, kc, 128:192], CQ[:, kc, ts_], start=(kc == 0), stop=(kc == 2))
                for kc in range(3):
                    b.mm(bk2[0:64, :], WUQ[sl][:, kc, 192:256], CQ[:, kc, ts_], start=(kc == 0), stop=(kc == 2))
                rope(bk1[0:64, :], bk2[0:64, :], ts_, QR[:, ts_], SC_MLA)
                bk = rotA.next()
                for kc in range(2):
                    b.mm(bk, WUKV[sl][:, kc, 0:128], CKV[:, kc, ts_], start=(kc == 0), stop=(kc == 1))
                b.copy('act', KN[:, ts_], bk)
            for g in range(4):
                bk = rotA.next()
                for j in range(4):
                    tb = g * 4 + j
                    for kc in range(2):
                        b.mm(bk[:, j * 128:(j + 1) * 128], CKV[:, kc, tb * 128:(tb + 1) * 128],
                             WUKV[sl][:, kc, 128:256], start=(kc == 0), stop=(kc == 1))
                b.copy('dve', V[:, g * 4:(g + 1) * 4, :], bk.rearrange("p (j d) -> p j d", j=4))
            if h + 2 < 8:
                loadh(h + 2)
            pairs = []
            for qc in range(4):
                for kb in range(4 * qc + 4):
                    pairs.append((qc, kb))
            acc = {}
            pendq = []

            def pv(qc, kb, pt, c0):
                nkb = 4 * qc + 4
                OTa, DEN = acc[qc]
                b.add('pe', lambda e: e.matmul(OTa[:, c0:], V[:, kb, :], pt[:, c0:], start=(kb == 0),
                                               stop=(kb == nkb - 1), skip_group_check=True),
                      reads=[V[:, kb, :], pt[:, c0:]], writes=[OTa[:, c0:]])
                b.add('pe', lambda e: e.matmul(DEN[:, c0:], ONES, pt[:, c0:], start=(kb == 0),
                                               stop=(kb == nkb - 1), skip_group_check=True),
                      reads=[ONES, pt[:, c0:]], writes=[DEN[:, c0:]])
                if kb == nkb - 1:
                    q0 = qc * 512
                    rd = RDEN[qc % 2]
                    b.act(rd, DEN, AF.Ln)
                    b.act(rd, rd, AF.Exp, scale=-1.0)
                    b.tt('dve', OT[:, h, q0:q0 + 512], OTa, rd, ALU.mult)

            for (qc, kb) in pairs:
                if kb == 0:
                    acc[qc] = (rotO.next(), rotO.next())
                q0 = qc * 512
                j = kb - 4 * qc
                c0 = max(j, 0) * 128
                Z = rotZ.next()
                ks = slice(kb * 128, (kb + 1) * 128)
                b.mm(Z[:, c0:], KN[:, ks], QN[:, q0 + c0:q0 + 512], start=True, stop=False)
                b.mm(Z[:, c0:], KPE[:, ks], QR[:, q0 + c0:q0 + 512], start=False, stop=(j < 0))
                if j >= 0:
                    b.mm(Z[:, c0:c0 + 128], IDB, NMI, start=False, stop=True)
                pt = PT[npt % 4]
                npt += 1
                b.act(pt[:, c0:], Z[:, c0:], AF.Exp)
                pendq.append((qc, kb, pt, c0))
                if len(pendq) > 2:
                    pv(*pendq.pop(0))
            while pendq:
                pv(*pendq.pop(0))
        out_proj(WMO, 8, OT, rotAll, tail)
        A.release(m0)

    def xattn_phase(i, tail):
        m0 = A.mark()
        WQ = A.alloc([8, 512], BF16)
        WKV = A.alloc([8, 1024], BF16)
        MEMT = A.alloc([8, 256], F32)
        MN = A.alloc([8, 256], BF16)
        MEMS = A.alloc([2, 1024], F32)
        WXO = A.alloc([4, 1024], BF16)
        wload(WKV, xkv[i])
        wload(WQ, xq[i])
        wload(WXO, xo[i])
        KMT = A.alloc([4, 256], BF16)
        VM = A.alloc([2, 512], BF16)
        QX = A.alloc([4, S], BF16)
        OX = QX
        PT = [A.alloc([512], BF16) for _ in range(6)]
        RDEN = [A.alloc([512], F32) for _ in range(2)]
        b.dma('sp', MEMS, mem.rearrange("(t p) n -> p t n", p=128))
        for t in range(2):
            for half in range(2):
                bk = rotAll.next()
                for j in range(4):
                    c = half * 4 + j
                    b.tr(bk[:, j * 128:(j + 1) * 128], MEMS[:, t, c * 128:(c + 1) * 128], IDF)
                b.copy('act', MEMT[:, half * 4:(half + 1) * 4, t * 128:(t + 1) * 128],
                       bk.rearrange("p (c t) -> p c t", c=4))
        rmsnorm(MEMT, 8, G_MEM[i], MN, 256, 1024, rotAll, sq_eng='act')
        for hh in range(4):
            bk = rotAll.next()
            for kc in range(8):
                b.mm(bk[:, 0:256], WKV[:, kc, hh * 128:(hh + 1) * 128], MN[:, kc, :], start=(kc == 0), stop=(kc == 7))
            b.copy('act', KMT[:, hh, :], bk[:, 0:256])
        for mt in range(2):
            bk = rotAll.next()
            for kc in range(8):
                b.mm(bk, MN[:, kc, mt * 128:(mt + 1) * 128], WKV[:, kc, 512:1024], start=(kc == 0), stop=(kc == 7))
            b.copy('dve', VM[:, mt, :], bk)
        rotQ = Rot([bank[3]])

        def qproj(hh, tc):
            ts_ = slice(tc * 512, (tc + 1) * 512)
            bk = rotQ.next()
            for kc in range(8):
                b.mm(bk, WQ[:, kc, hh * 128:(hh + 1) * 128], AN[:, kc, ts_], start=(kc == 0), stop=(kc == 7))
            actmul(QX[:, hh, ts_], bk, SC_XA)

        for hh in range(4):
            qproj(hh, 0)
        rotZ = Rot([bank[0], bank[1], bank[2]])
        rotO = Rot([bank[4], bank[5], bank[6], bank[7]])
        npt = 0
        groups = [(hh, tc) for tc in range(4) for hh in range(4)]
        gst = {}

        def xa_front(g):
            nonlocal npt
            hh, tc = groups[g]
            ts_ = slice(tc * 512, (tc + 1) * 512)
            pts = []
            for mt in range(2):
                Z = rotZ.next()
                b.mm(Z, KMT[:, hh, mt * 128:(mt + 1) * 128], QX[:, hh, ts_], start=True, stop=True)
                pt = PT[npt % 6]
                npt += 1
                b.act(pt, Z, AF.Exp)
                pts.append(pt)
            gst[g] = pts

        def xa_back(g):
            hh, tc = groups[g]
            ts_ = slice(tc * 512, (tc + 1) * 512)
            pts = gst.pop(g)
            OTa = rotO.next()
            DEN = rotO.next()
            for mt in range(2):
                b.mm(DEN, ONES, pts[mt], start=(mt == 0), stop=(mt == 1))
            for mt in range(2):
                b.mm(OTa, VM[:, mt, hh * 128:(hh + 1) * 128], pts[mt], start=(mt == 0), stop=(mt == 1))
            rd = RDEN[g % 2]
            b.act(rd, DEN, AF.Ln)
            b.act(rd, rd, AF.Exp, scale=-1.0)
            b.tt('dve', OX[:, hh, ts_], OTa, rd, ALU.mult)

        for g in range(len(groups) + 1):
            if g < len(groups):
                xa_front(g)
                hh, tc = groups[g]
                if tc + 1 < 4:
                    if tc + 1 == 3 and hh == 0:
                        flush_pending()
                    qproj(hh, tc + 1)
            if g >= 1:
                xa_back(g - 1)
        pw_in = A.view(m0, [8, 1024], BF16)
        pw_out = A.view(m0 + 16384, [8, 1024], BF16)
        wload(pw_in, wi[i][:, 0:1024])
        wload(pw_out, wo[i][0:1024, :])
        mlp_pre[i] = True
        out_proj(WXO, 4, OX, rotAll, tail)
        A.release(m0)

    def mlp_phase(i, tail):
        m0 = A.mark()
        los = [None] * 4
        WIN = [None, None]
        WOUT = [None, None]
        WIN[0] = A.alloc([8, 1024], BF16)
        los[0] = A.last_lo
        WOUT[0] = A.alloc([8, 1024], BF16)
        los[2] = A.last_lo
        WIN[1] = A.alloc([8, 1024], BF16)
        los[1] = A.last_lo
        WOUT[1] = A.alloc([8, 1024], BF16)
        los[3] = A.last_lo
        fin_state['los'] = los
        R = [A.alloc([8, 512], BF16) for _ in range(2)]
        RL = [A.alloc([512], F32) for _ in range(3)]

        def loadg(fg):
            sl = fg % 2
            wload(WIN[sl], wi[i][:, fg * 1024:(fg + 1) * 1024])
            wload(WOUT[sl], wo[i][fg * 1024:(fg + 1) * 1024, :])
        if not mlp_pre.get(i):
            loadg(0)
        loadg(1)
        nrl = [0]
        blocks = [(fg, tc) for fg in range(4) for tc in range(4)]

        def u_block(k):
            fg, tc = blocks[k]
            sl = fg % 2
            ts_ = slice(tc * 512, (tc + 1) * 512)
            r = R[k % 2]
            for fc in range(8):
                bk = rotAll.next()
                for kc in range(8):
                    b.mm(bk, WIN[sl][:, kc, fc * 128:(fc + 1) * 128], AN[:, kc, ts_], start=(kc == 0), stop=(kc == 7))
                rl = RL[nrl[0] % 3]
                nrl[0] += 1
                b.act(rl, bk, AF.Relu)
                b.tt('pool', r[:, fc, :], rl, rl, ALU.mult)

        def y_block(k):
            fg, tc = blocks[k]
            sl = fg % 2
            ts_ = slice(tc * 512, (tc + 1) * 512)
            r = R[k % 2]
            for n in range(8):
                bk = rotAll.next()
                for fc in range(8):
                    b.mm(bk, WOUT[sl][:, fc, n * 128:(n + 1) * 128], r[:, fc, :], start=(fc == 0), stop=(fc == 7))
                b.tt('dve', HT[:, n, ts_], bk, HT[:, n, ts_], ALU.add)

        u_block(0)
        for k in range(16):
            fg, tc = blocks[k]
            if k + 1 < 16:
                u_block(k + 1)
            y_block(k)
            if fg == 0 and tc == 0:
                flush_pending()
            if fg == 3 and tc >= 1:
                tail(tc - 1)
            if tc == 3 and fg + 2 < 4:
                loadg(fg + 2)
        pending.append(lambda: tail(3))
        A.release(m0)

    def sb_phase(tail):
        m0 = A.mark()
        OT = A.alloc([8, S], BF16)
        WSO = A.alloc([8, 1024], BF16)
        WH = [A.alloc([8, 384], BF16) for _ in range(2)]
        QS = [A.alloc([S], BF16) for _ in range(2)]
        KS = [A.alloc([S], BF16) for _ in range(2)]
        VS = [A.alloc([16, 128], BF16) for _ in range(2)]
        E = [A.alloc([512], F32) for _ in range(2)]
        LB = [A.alloc([512], BF16) for _ in range(3)]
        ARG = [A.alloc([512], F32) for _ in range(2)]
        AT = [A.alloc([512], BF16) for _ in range(2)]
        RS = [A.alloc([512], F32) for _ in range(2)]
        rs_state = {}

        def loadh(h):
            sl = h % 2
            for part in range(3):
                wload(WH[sl][:, :, part * 128:(part + 1) * 128], w_qkv[:, part * 1024 + h * 128:part * 1024 + (h + 1) * 128])
        loadh(0)
        loadh(1)
        wload(WSO, w_so)
        rotZ = Rot([bank[0], bank[1], bank[2], bank[3]])
        rotC = Rot([bank[4], bank[5]])
        OTA = bank[6]
        PB = bank[7]

        def proj_steps(h):
            sl = h % 2
            steps = []
            for tc in range(4):
                ts_ = slice(tc * 512, (tc + 1) * 512)
                for which in range(2):
                    for half in range(2):
                        def f(tc=tc, ts_=ts_, which=which, half=half):
                            for kc in range(half * 4, half * 4 + 4):
                                b.mm(PB, WH[sl][:, kc, which * 128:(which + 1) * 128], AN[:, kc, ts_],
                                     start=(kc == 0), stop=(kc == 7))
                            if half == 1:
                                if which == 0:
                                    b.ts('dve', QS[sl][:, ts_], PB, float(SC_SB), None, ALU.mult)
                                else:
                                    b.copy('dve', KS[sl][:, ts_], PB)
                        steps.append(f)
            for g in range(4):
                for half in range(2):
                    def f(g=g, half=half):
                        for j in range(half * 2, half * 2 + 2):
                            tb = g * 4 + j
                            for kc in range(8):
                                b.mm(PB[:, j * 128:(j + 1) * 128], AN[:, kc, tb * 128:(tb + 1) * 128],
                                     WH[sl][:, kc, 256:384], start=(kc == 0), stop=(kc == 7))
                        if half == 1:
                            b.copy('dve', VS[sl][:, g * 4:(g + 1) * 4, :], PB.rearrange("p (j d) -> p j d", j=4))
                    steps.append(f)
            return steps

        st0 = proj_steps(0)
        qk = st0[0:16]
        vv = st0[16:24]
        for f in qk[0:12] + vv[0:6]:
            f()
        flush_pending()
        for f in qk[12:16] + vv[6:8]:
            f()
        loadh_pending = None
        for h in range(8):
            sl = h % 2
            Qh, Kh, Vh = QS[sl], KS[sl], VS[sl]
            nxt = proj_steps(h + 1) if h + 1 < 8 else []
            pairs = []
            for qc in range(4):
                for kb in reversed(range(4 * qc + 4)):
                    pairs.append((qc, kb))
            st = {}

            def stageZ(t):
                qc, kb = pairs[t]
                q0 = qc * 512
                j = kb - 4 * qc
                c0 = max(j, 0) * 128
                Z = rotZ.next()
                ks = slice(kb * 128, (kb + 1) * 128)
                b.mm(Z[:, c0:], Kh[:, ks], Qh[:, q0 + c0:q0 + 512], start=True, stop=(j < 0))
                if j >= 0:
                    b.mm(Z[:, c0:c0 + 128], IDB, NMS, start=False, stop=True)
                st[t] = dict(Z=Z, c0=c0, qc=qc, kb=kb)

            def stageE(t):
                s_ = st[t]
                c0, Z = s_['c0'], s_['Z']
                e = E[t % 2]
                b.act(e[:, c0:], Z[:, c0:], AF.Exp)
                s_['e'] = e

            def stageA2(t):
                s_ = st[t]
                c0 = s_['c0']
                lb = LB[t % 3]
                b.act(lb[:, c0:], s_['e'][:, c0:], AF.Ln, bias=1.0)
                s_['lb'] = lb

            def stageTRI(t):
                s_ = st[t]
                c0, Z, lb = s_['c0'], s_['Z'], s_['lb']
                CS = rotC.next()
                s_['CS'] = CS
                b.add('pe', lambda e: e.matmul(Z[:, c0:], NEGTRI, lb[:, c0:], start=False, stop=True,
                                               skip_group_check=True),
                      reads=[NEGTRI, lb[:, c0:]], writes=[Z[:, c0:]])
                b.mm(CS[:, c0:], NEGONES, lb[:, c0:], start=True, stop=True)

            def stageARG(t):
                s_ = st[t]
                c0, Z, CS, qc, kb = s_['c0'], s_['Z'], s_['CS'], s_['qc'], s_['kb']
                nkb = 4 * qc + 4
                arg = ARG[t % 2]
                if kb == nkb - 1:
                    rs_state['cur'] = 0
                    b.copy('dve', arg[:, c0:], Z[:, c0:])
                    b.copy('dve', RS[0][:, c0:], CS[:, c0:])
                    if c0 > 0:
                        b.memset('pool', RS[0][:, 0:c0], 0.0)
                else:
                    cur = rs_state['cur']
                    rs = RS[cur]
                    rn = RS[1 - cur]
                    b.tt('dve', arg[:, c0:], Z[:, c0:], rs[:, c0:], ALU.add)
                    b.tt('dve', rn[:, c0:], CS[:, c0:], rs[:, c0:], ALU.add)
                    if c0 > 0:
                        b.copy('pool', rn[:, 0:c0], rs[:, 0:c0])
                    rs_state['cur'] = 1 - cur
                s_['arg'] = arg

            def stageB2(t):
                s_ = st[t]
                c0 = s_['c0']
                at = AT[t % 2]
                b.act(at[:, c0:], s_['arg'][:, c0:], AF.Exp)
                s_['at'] = at

            def stageC(t):
                s_ = st[t]
                c0, qc, kb, at = s_['c0'], s_['qc'], s_['kb'], s_['at']
                nkb = 4 * qc + 4
                vblk = Vh[:, kb, :]
                b.add('pe', lambda e: e.matmul(OTA[:, c0:], vblk, at[:, c0:], start=(kb == nkb - 1),
                                               stop=(kb == 0), skip_group_check=True),
                      reads=[vblk, at[:, c0:]], writes=[OTA[:, c0:]])
                if kb == 0:
                    q0 = qc * 512
                    b.copy('dve', OT[:, h, q0:q0 + 512], OTA)
                del st[t]

            NPR = len(pairs)
            for t in range(NPR + 6):
                if t < NPR:
                    stageZ(t)
                if 0 <= t - 1 < NPR:
                    stageE(t - 1)
                if 0 <= t - 4 < NPR:
                    stageB2(t - 4)
                if 0 <= t - 1 < NPR:
                    stageA2(t - 1)
                if nxt and t >= 1 and (t * 24) // 38 > ((t - 1) * 24) // 38:
                    nxt.pop(0)()
                    if not nxt and h + 2 < 8:
                        loadh(h + 2)
                if 0 <= t - 2 < NPR:
                    stageTRI(t - 2)
                if 0 <= t - 3 < NPR:
                    stageARG(t - 3)
                if 0 <= t - 5 < NPR:
                    stageC(t - 5)
            while nxt:
                nxt.pop(0)()
        out_proj(WSO, 8, OT, rotAll, tail)
        A.release(m0)

    def final_tail(tc):
        if 'OUTF' not in fin_state:
            los = fin_state['los']
            fin_state['OUTF'] = [A.view(los[0], [8, 512], F32)]
            fin_state['OST'] = [A.view(los[2], [1024], F32), A.view(los[2] + 4096, [1024], F32)]
        OUTF, OST = fin_state['OUTF'], fin_state['OST']
        ts_ = slice(tc * 512, (tc + 1) * 512)
        of = OUTF[0]
        rmsnorm(HT[:, :, ts_], 8, G_FINAL, of, 512, 1024, rotAll, sq_eng='mix')
        for tbl in range(4):
            tb = tc * 4 + tbl
            ost = OST[tb % 2]
            for half in range(2):
                bk = rotAll.next()
                for j in range(4):
                    c = half * 4 + j
                    b.tr(bk[:, j * 128:(j + 1) * 128], of[:, c, tbl * 128:(tbl + 1) * 128], IDF)
                b.copy('act' if half == 0 else 'dve', ost[:, half * 512:(half + 1) * 512], bk)
            b.dma('sp', out[tb * 128:(tb + 1) * 128, :], ost, is_output=True)

    def no_tail(tc):
        pass

    def tail_for(k):
        if stage < 99 and k >= stage:
            return no_tail
        return [None, norm_tail(G_CROSS[0]), norm_tail(G_MLP[0]), norm_tail(G_MIX[1]),
                norm_tail(G_CROSS[1]), norm_tail(G_MLP[1]), final_tail][k]

    if stage >= 1:
        mla_phase(tail_for(1), COS, SINS, m_mla)
    if stage >= 2:
        xattn_phase(0, tail_for(2))
    if stage >= 3:
        mlp_phase(0, tail_for(3))
    if stage >= 4:
        sb_phase(tail_for(4))
    if stage >= 5:
        xattn_phase(1, tail_for(5))
    if stage >= 6:
        mlp_phase(1, tail_for(6))
    flush_pending()
    if stage < 99:
        b.dma('sp', dbg.rearrange("p (c t) -> p c t", c=8), HT, is_output=True)
    info = b.emit()
    return nc, info, A.peak


def make_consts():
    j = np.arange(128)[:, None]
    k = np.arange(128)[None, :]
    idf = np.eye(128, dtype=np.float32)
    ones = np.ones((128, 128), np.float32)
    negtri = np.where(j >= k, -1.0, 0.0).astype(np.float32)
    nms = np.where(j >= k, NEGV, 0.0).astype(np.float32)
    nmi = np.where(j > k, NEGV, 0.0).astype(np.float32)
    return np.ascontiguousarray(np.concatenate([idf, idf, ones, -ones, negtri, nms, nmi], axis=1))


def colmaj(g):
    g = np.asarray(g, np.float32)
    return g.reshape(-1, 128).T


def make_gv(inp):
    gvv = np.zeros((128, NG), np.float32)
    for i in range(2):
        gvv[:, G_MIX[i]:G_MIX[i] + 8] = colmaj(inp["norm_mix"][i])
        gvv[:, G_CROSS[i]:G_CROSS[i] + 8] = colmaj(inp["norm_cross"][i])
        gvv[:, G_MEM[i]:G_MEM[i] + 8] = colmaj(inp["norm_mem"][i])
        gvv[:, G_MLP[i]:G_MLP[i] + 8] = colmaj(inp["norm_mlp"][i])
    gvv[:, G_FINAL:G_FINAL + 8] = colmaj(inp["norm_final"])
    gvv[:, G_Q:G_Q + 3] = colmaj(inp["mla_g_q"][0])
    gvv[:, G_KV:G_KV + 2] = colmaj(inp["mla_g_kv"][0])
    invf = (np.float32(10000.0) ** (-np.arange(0, 64, 2, dtype=np.float32) / np.float32(64))).astype(np.float32)
    gvv[0:32, G_INVF] = invf
    gvv[32:64, G_INVF] = invf
    return gvv


def make_in_maps(inp, cores):
    f = lambda a: np.ascontiguousarray(np.asarray(a, dtype=np.float32))
    shared = dict(
        gv=make_gv(inp), cst=make_consts(),
        w_dkv=f(inp["mla_w_dkv"][0]), w_uq=f(inp["mla_w_uq"][0]), w_ukv=f(inp["mla_w_ukv"][0]),
        w_mo=f(inp["mla_w_o"][0]), w_qkv=f(inp["sb_w_qkv"][0]), w_so=f(inp["sb_w_o"][0]),
    )
    for i in range(2):
        shared[f"xq{i}"] = f(inp["xa_w_q"][i])
        shared[f"xkv{i}"] = f(inp["xa_w_kv"][i])
        shared[f"xo{i}"] = f(inp["xa_w_o"][i])
        shared[f"wi{i}"] = f(inp["mlp_w_in"][i])
        shared[f"wo{i}"] = f(inp["mlp_w_out"][i])
    maps = []
    for c in cores:
        m = dict(shared)
        m["x"] = f(inp["x"][c])
        m["mem"] = f(inp["mem"][c])
        m["pos"] = np.ascontiguousarray(np.asarray(inp["positions"][c], dtype=np.int32).reshape(1, S))
        maps.append(m)
    return maps


def kernel(**inputs):
    nc, info, peak = build_program(99)
    maps = make_in_maps(inputs, list(range(8)))
    res = run_bass_kernel_spmd(nc, maps, core_ids=list(range(8)))
    return np.stack([np.asarray(r["out"], dtype=np.float32) for r in res.results], axis=0)
```

```python
import numpy as np
import concourse.bass as bass
import concourse.mybir as mybir

F32 = mybir.dt.float32
BF16 = mybir.dt.bfloat16
I32 = mybir.dt.int32
AF = mybir.ActivationFunctionType
ALU = mybir.AluOpType
AX = mybir.AxisListType
DSZ = {F32: 4, BF16: 2, I32: 4}
BLK = 256
ATTACH_WAIT = 1


def ap_range(ap):
    sp = str(ap.space)
    if 'SB' in sp.upper() or 'STATE' in sp.upper():
        space = 'sb'
    elif 'PSUM' in sp.upper():
        space = 'ps'
    else:
        return None
    dsz = DSZ[ap.dtype]
    pat = ap.ap
    pstep = pat[0][0]
    off = ap.offset % pstep if pstep > 0 else ap.offset
    ext = 1
    for st, cnt in pat[1:]:
        ext += (cnt - 1) * abs(st)
    lo, hi = off * dsz, (off + ext) * dsz
    if space == 'ps':
        lo = lo // 2048 * 2048
        hi = (hi + 2047) // 2048 * 2048
    return space, lo, hi


class Op:
    __slots__ = ('eng', 'fn', 'deps', 'needs_inc', 'inc_val', 'is_dma', 'dsem', 'dval', 'idx', 'name', 'seq', 'Kc')

    def __init__(self, eng, fn, is_dma=False, name=''):
        self.eng = eng
        self.fn = fn
        self.deps = []
        self.needs_inc = False
        self.inc_val = None
        self.is_dma = is_dma
        self.dsem = None
        self.dval = None
        self.name = name


class Tracker:
    def __init__(self, nbytes):
        n = (nbytes + BLK - 1) // BLK
        self.w = [None] * n
        self.r = [None] * n

    def access(self, op, lo, hi, write):
        deps = []
        for b in range(lo // BLK, (hi - 1) // BLK + 1):
            w = self.w[b]
            if write:
                if w is not None:
                    deps.append((w, 'WAW'))
                rs = self.r[b]
                if rs:
                    for r in rs.values():
                        deps.append((r, 'WAR'))
                self.w[b] = op
                self.r[b] = None
            else:
                if w is not None:
                    deps.append((w, 'RAW'))
                rs = self.r[b]
                if rs is None:
                    rs = self.r[b] = {}
                key = id(op) if op.is_dma else op.eng
                rs[key] = op
        return deps


class Builder:
    ENGS = ['pe', 'act', 'dve', 'pool', 'sp']

    def __init__(self, nc, n_dma_sems=6):
        self.nc = nc
        self.ops = {e: [] for e in self.ENGS}
        self.trk = {'sb': Tracker(nc.SBUF_PARTITION_SIZE_BYTES), 'ps': Tracker(16384)}
        self.n_dma_sems = n_dma_sems
        self.dma_count = {}
        self.dma_hist = {}
        self.out_dmas = []
        self.nseq = 0

    def add(self, eng, fn, reads=(), writes=(), is_dma=False, name=''):
        op = Op(eng, fn, is_dma, name)
        seen = set()
        for aps, wr in ((reads, False), (writes, True)):
            for ap in aps:
                if ap is None or isinstance(ap, (int, float)):
                    continue
                r = ap_range(ap)
                if r is None:
                    continue
                for (p, kind) in self.trk[r[0]].access(op, r[1], r[2], wr or r[0] == 'ps'):
                    if p is op:
                        continue
                    if (not p.is_dma) and (not op.is_dma) and p.eng == eng:
                        if eng == 'pe':
                            continue
                    if id(p) in seen:
                        continue
                    seen.add(id(p))
                    op.deps.append(p)
        if is_dma:
            q = eng
            n = self.dma_count.get(q, 0)
            k = n % self.n_dma_sems
            prev = self.dma_hist.get((q, k))
            if prev is not None and id(prev) not in seen:
                op.deps.append(prev)
            self.dma_hist[(q, k)] = op
            self.dma_count[q] = n + 1
            op.dsem = (q, k)
            op.dval = 16 * (n // self.n_dma_sems + 1)
        op.seq = self.nseq
        self.nseq += 1
        op.idx = len(self.ops[eng])
        self.ops[eng].append(op)
        return op

    def mm(self, out, lhsT, rhs, start=True, stop=True):
        return self.add('pe', lambda e: e.matmul(out, lhsT, rhs, start=start, stop=stop),
                        reads=[lhsT, rhs], writes=[out])

    def tr(self, out, in_, ident):
        return self.add('pe', lambda e: e.transpose(out, in_, ident), reads=[in_, ident], writes=[out])

    def act(self, out, in_, func, bias=0.0, scale=1.0):
        rd = [in_]
        if not isinstance(bias, (int, float)):
            rd.append(bias)
        if not isinstance(scale, (int, float)):
            rd.append(scale)
        return self.add('act', lambda e: e.activation(out, in_, func, bias=bias, scale=scale),
                        reads=rd, writes=[out])

    def tt(self, eng, out, in0, in1, op):
        return self.add(eng, lambda e: e.tensor_tensor(out, in0, in1, op), reads=[in0, in1], writes=[out])

    def ts(self, eng, out, in0, s1, s2, op0, op1=None):
        rd = [in0]
        for s in (s1, s2):
            if s is not None and not isinstance(s, (int, float)):
                rd.append(s)
        if op1 is None:
            return self.add(eng, lambda e: e.tensor_scalar(out, in0, s1, None, op0), reads=rd, writes=[out])
        return self.add(eng, lambda e: e.tensor_scalar(out, in0, s1, s2, op0, op1), reads=rd, writes=[out])

    def stt(self, eng, out, in0, scalar, in1, op0, op1):
        rd = [in0, in1]
        if not isinstance(scalar, (int, float)):
            rd.append(scalar)
        return self.add(eng, lambda e: e.scalar_tensor_tensor(out, in0, scalar, in1, op0, op1),
                        reads=rd, writes=[out])

    def copy(self, eng, out, in_):
        if eng == 'act':
            return self.act(out, in_, AF.Copy)
        return self.add(eng, lambda e: e.tensor_copy(out, in_), reads=[in_], writes=[out])

    def memset(self, eng, out, val):
        return self.add(eng, lambda e: e.memset(out, val), writes=[out])

    def dma(self, q, out, in_, is_output=False):
        op = self.add(q, lambda e: e.dma_start(out=out, in_=in_), reads=[in_], writes=[out], is_dma=True)
        if is_output:
            self.out_dmas.append(op)
        return op

    def emit(self):
        nc = self.nc
        fin = Op('sp', None)
        fin.deps = list(self.out_dmas)
        fin.seq = self.nseq
        fin.idx = len(self.ops['sp'])
        self.ops['sp'].append(fin)
        allops = sorted((op for e in self.ENGS for op in self.ops[e]), key=lambda o: o.seq)
        lastK = {e: {} for e in self.ENGS}
        n_before = n_after = 0
        for op in allops:
            K = dict(lastK[op.eng])
            surv = []
            for p in sorted(op.deps, key=lambda q: -q.seq):
                n_before += 1
                if p.is_dma:
                    if K.get(('d',) + p.dsem, -1) >= p.dval:
                        continue
                else:
                    if K.get(p.eng, -1) >= p.idx:
                        continue
                surv.append(p)
                for k, v in p.Kc.items():
                    if K.get(k, -1) < v:
                        K[k] = v
            n_after += len(surv)
            op.deps = surv
            lastK[op.eng] = K
            kc = dict(K)
            if op.is_dma:
                kc[('d',) + op.dsem] = op.dval
            else:
                kc[op.eng] = max(kc.get(op.eng, -1), op.idx)
            op.Kc = kc
        self.prune_stats = (n_before, n_after)
        for e in self.ENGS:
            for op in self.ops[e]:
                for p in op.deps:
                    if not p.is_dma:
                        p.needs_inc = True
        cnt = {}
        for e in self.ENGS:
            c = 0
            for op in self.ops[e]:
                if op.needs_inc and not op.is_dma:
                    c += 1
                    op.inc_val = c
            cnt[e] = c
        self.sem_counts = cnt
        sems = {e: nc.alloc_semaphore(name=f"s_{e}") for e in self.ENGS}
        dsems = {}
        for (q, k) in self.dma_hist.keys():
            dsems[(q, k)] = nc.alloc_semaphore(name=f"d_{q}{k}")
        engobj = {'pe': 'tensor', 'act': 'scalar', 'dve': 'vector', 'pool': 'gpsimd', 'sp': 'sync'}

        def run(e, eng):
            waited = {}
            for op in self.ops[e]:
                need = {}
                for p in op.deps:
                    if p.is_dma:
                        key, val, sem = ('d',) + p.dsem, p.dval, dsems[p.dsem]
                    else:
                        key, val, sem = ('c', p.eng), p.inc_val, sems[p.eng]
                    if waited.get(key, 0) >= val:
                        continue
                    if key not in need or need[key][1] < val:
                        need[key] = (sem, val)
                items = list(need.items())
                attach = []
                while items and op.fn is not None and len(attach) < ATTACH_WAIT:
                    attach.append(items.pop())
                for key, (sem, val) in items:
                    waited[key] = val
                    eng.wait_ge(sem, val)
                if op.fn is None:
                    continue
                ins = op.fn(eng)
                for key, (sem, val) in attach:
                    waited[key] = val
                    ins._wait_ge(sem, val)
                if op.is_dma:
                    ins.then_inc(dsems[op.dsem], 16)
                elif op.needs_inc:
                    ins.then_inc(sems[e], 1)

        with nc.Block() as block:
            @block.tensor
            def _(eng):
                run('pe', eng)

            @block.scalar
            def _(eng):
                run('act', eng)

            @block.vector
            def _(eng):
                run('dve', eng)

            @block.gpsimd
            def _(eng):
                run('pool', eng)

            @block.sync
            def _(eng):
                run('sp', eng)
        return {e: len(self.ops[e]) for e in self.ENGS}, cnt


class Arena:
    def __init__(self, nc, nbytes):
        self.nc = nc
        self.n = nbytes
        self.t = nc.alloc_sbuf_tensor("arena", [128, nbytes // 4], F32)
        self.top = 0
        self.peak = 0

    def mark(self):
        return self.top

    def release(self, m):
        self.top = m

    def alloc(self, shape_free, dtype, parts=128):
        n = int(np.prod(shape_free)) * DSZ[dtype]
        n4 = (n + 255) // 256 * 256
        lo = self.top
        self.top += n4
        self.peak = max(self.peak, self.top)
        assert self.top <= self.n, f"arena overflow {self.top} > {self.n}"
        self.last_lo = lo
        return self.view(lo, shape_free, dtype, parts)

    def view(self, lo, shape_free, dtype, parts=128):
        n = int(np.prod(shape_free)) * DSZ[dtype]
        n4 = (n + 255) // 256 * 256
        v = self.t[0:parts, lo // 4:(lo + n4) // 4]
        if dtype != F32:
            v = v.bitcast(dtype)
        v = v[:, 0:int(np.prod(shape_free))]
        if len(shape_free) == 2:
            v = v.rearrange("p (a b) -> p a b", a=shape_free[0])
        elif len(shape_free) == 3:
            v = v.rearrange("p (a b c) -> p a b c", a=shape_free[0], b=shape_free[1])
        return v


from concourse.bass_utils import run_bass_kernel_spmd

S = 2048
D = 1024
SC_MLA = 192 ** -0.5
SC_SB = 128 ** -0.5
SC_XA = 128 ** -0.5
EPS = 1e-6
NEGV = -30000.0
G_MIX = [0, 32]
G_CROSS = [8, 40]
G_MEM = [16, 48]
G_MLP = [24, 56]
G_FINAL = 64
G_Q = 72
G_KV = 75
G_INVF = 77
NG = 80
C1 = 6.28125
C2 = float(2 * np.pi - 6.28125)
PI_LO = 3.1415925


class Rot:
    def __init__(self, items):
        self.items = list(items)
        self.i = 0

    def next(self):
        v = self.items[self.i % len(self.items)]
        self.i += 1
        return v


def build_program(stage=99):
    nc = bass.Bass("TRN2", target_bir_lowering=False)

    def din(name, shape, dt=F32):
        return nc.dram_tensor(name, shape, dt, kind="ExternalInput").ap()

    x = din("x", [S, D])
    mem = din("mem", [256, D])
    pos = din("pos", [1, S], I32)
    gv = din("gv", [128, NG])
    cst = din("cst", [128, 7 * 128])
    w_dkv = din("w_dkv", [1024, 704])
    w_uq = din("w_uq", [384, 1536])
    w_ukv = din("w_ukv", [256, 2048])
    w_mo = din("w_mo", [1024, 1024])
    w_qkv = din("w_qkv", [1024, 3072])
    w_so = din("w_so", [1024, 1024])
    xq = [din(f"xq{i}", [1024, 512]) for i in range(2)]
    xkv = [din(f"xkv{i}", [1024, 1024]) for i in range(2)]
    xo = [din(f"xo{i}", [512, 1024]) for i in range(2)]
    wi = [din(f"wi{i}", [1024, 4096]) for i in range(2)]
    wo = [din(f"wo{i}", [4096, 1024]) for i in range(2)]
    if stage >= 99:
        out = nc.dram_tensor("out", [S, D], F32, kind="ExternalOutput").ap()
    else:
        dbg = nc.dram_tensor("dbg", [128, 8 * S], F32, kind="ExternalOutput").ap()

    A = Arena(nc, 212480)
    PS = nc.alloc_psum_tensor("ps", [128, 4096], F32)
    bank = [PS[:, i * 512:(i + 1) * 512] for i in range(8)]
    b = Builder(nc)

    def wload(dst, src):
        b.dma('pool', dst, src.rearrange("(c p) n -> p c n", p=128))

    HT = A.alloc([8, S], F32)
    AN = A.alloc([8, S], BF16)
    IDF = A.alloc([128], F32)
    CB = A.alloc([6, 128], BF16)
    GV = A.alloc([NG], F32)
    EPSC = A.alloc([1], F32)
    SQ = [A.alloc([512], BF16) for _ in range(3)]
    RSTD = [A.alloc([512], F32) for _ in range(2)]
    IDB = CB[:, 0, :]
    ONES = CB[:, 1, :]
    NEGONES = CB[:, 2, :]
    NEGTRI = CB[:, 3, :]
    NMS = CB[:, 4, :]
    NMI = CB[:, 5, :]
    b.dma('sp', IDF, cst[:, 0:128])
    b.dma('sp', GV, gv)
    b.dma('pool', CB, cst[:, 128:896].rearrange("p (a c) -> p a c", a=6))
    b.memset('dve', EPSC, EPS)
    cnt = {'sq': 0, 'rs': 0}
    fin_state = {}
    pending = []
    mlp_pre = {}

    def rmsnorm(src, C, gcol, dst, T, nfeat, rot, sq_eng='pool'):
        for t0 in range(0, T, 512):
            n = min(512, T - t0)
            ss = rot.next()
            for c in range(C):
                sq = SQ[cnt['sq'] % 3]
                cnt['sq'] += 1
                if sq_eng == 'act' or (sq_eng == 'mix' and c % 2 == 0):
                    b.act(sq[:, :n], src[:, c, t0:t0 + n], AF.Square)
                else:
                    b.tt('pool', sq[:, :n], src[:, c, t0:t0 + n], src[:, c, t0:t0 + n], ALU.mult)
                b.mm(ss[:, :n], ONES, sq[:, :n], start=(c == 0), stop=(c == C - 1))
            rs = RSTD[cnt['rs'] % 2]
            cnt['rs'] += 1
            b.act(rs[:, :n], ss[:, :n], AF.Ln, bias=EPSC, scale=1.0 / nfeat)
            b.act(rs[:, :n], rs[:, :n], AF.Exp, scale=-0.5)
            for c in range(C):
                b.stt('dve', dst[:, c, t0:t0 + n], src[:, c, t0:t0 + n], GV[:, gcol + c:gcol + c + 1],
                      rs[:, :n], ALU.mult, ALU.mult)

    def actmul(out_, in_, c):
        b.add('act', lambda e: e.mul(out_, in_, c), reads=[in_], writes=[out_])

    def out_proj(W, KC, OTt, rot, tail):
        for tc in range(4):
            ts_ = slice(tc * 512, (tc + 1) * 512)
            for n in range(8):
                bk = rot.next()
                for hc in range(KC):
                    b.mm(bk, W[:, hc, n * 128:(n + 1) * 128], OTt[:, hc, ts_], start=(hc == 0), stop=(hc == KC - 1))
                b.tt('dve', HT[:, n, ts_], bk, HT[:, n, ts_], ALU.add)
            if tc >= 1:
                tail(tc - 1)
        pending.append(lambda: tail(3))

    def flush_pending():
        while pending:
            pending.pop(0)()

    def norm_tail(gcol):
        def f(tc):
            ts_ = slice(tc * 512, (tc + 1) * 512)
            rmsnorm(HT[:, :, ts_], 8, gcol, AN[:, :, ts_], 512, 1024, rotAll, sq_eng='act')
        return f

    rotAll = Rot(bank)

    m_mla = A.mark()
    COS = A.alloc([S], F32, parts=64)
    SINS = A.alloc([S], F32, parts=64)
    m0 = A.mark()
    XS = [A.alloc([1024], F32) for _ in range(6)]
    for tb in range(6):
        b.dma('sp', XS[tb], x[tb * 128:(tb + 1) * 128, :])
    m2 = A.mark()
    HS = S // 2
    TIs = [A.alloc([HS], I32, parts=64) for _ in range(2)]
    TA = A.alloc([HS], F32, parts=64)
    TK = A.alloc([HS], F32, parts=64)
    TR = A.alloc([HS], F32, parts=64)
    for hf in range(2):
        b.dma('sp', TIs[hf], pos[:, hf * HS:(hf + 1) * HS].partition_broadcast(64))
    for hf in range(2):
        hs_ = slice(hf * HS, (hf + 1) * HS)
        TI = TIs[hf]
        b.copy('dve', TA, TI)
        b.ts('dve', TA, TA, GV[0:64, G_INVF:G_INVF + 1], None, ALU.mult)
        b.ts('dve', TK, TA, float(1.0 / (2 * np.pi)), None, ALU.mult)
        b.copy('dve', TI, TK)
        b.copy('dve', TK, TI)
        b.stt('dve', TR, TK, -C1, TA, ALU.mult, ALU.add)
        b.stt('dve', TR, TK, -C2, TR, ALU.mult, ALU.add)
        b.ts('dve', SINS[:, hs_], TR, PI_LO, -PI_LO, ALU.min, ALU.max)
        b.ts('dve', TA, TR, float(np.pi / 2), None, ALU.add)
        b.ts('dve', TK, TA, float(np.pi), float(-2 * np.pi), ALU.is_gt, ALU.mult)
        b.tt('dve', TA, TA, TK, ALU.add)
        b.ts('dve', COS[:, hs_], TA, PI_LO, -PI_LO, ALU.min, ALU.max)
    A.release(m2)

    def rope_sin_tables():
        b.act(SINS[0:32, :], SINS[0:32, :], AF.Sin, scale=-1.0)
        b.act(SINS[32:64, :], SINS[32:64, :], AF.Sin)
        b.act(COS, COS, AF.Sin)

    for tb in range(16):
        xs = XS[tb % 6]
        for half in range(2):
            bk = rotAll.next()
            for j in range(4):
                c = half * 4 + j
                b.tr(bk[:, j * 128:(j + 1) * 128], xs[:, c * 128:(c + 1) * 128], IDF)
            b.copy('act', HT[:, half * 4:(half + 1) * 4, tb * 128:(tb + 1) * 128],
                   bk.rearrange("p (c t) -> p c t", c=4))
        if tb + 6 < 16:
            b.dma('sp', XS[tb % 6], x[(tb + 6) * 128:(tb + 7) * 128, :])
        if tb % 4 == 3 and tb >= 7 and stage >= 1:
            tcn = tb // 4 - 1
            tsn = slice(tcn * 512, (tcn + 1) * 512)
            rmsnorm(HT[:, :, tsn], 8, G_MIX[0], AN[:, :, tsn], 512, 1024, rotAll, sq_eng='act')
    A.release(m0)
    rope_sin_tables()

    def mla_phase(tail, COS, SINS, m0):
        CQ = A.alloc([3, S], BF16)
        CKV = A.alloc([2, S], BF16)
        KPE = A.alloc([S], BF16, parts=64)
        WMO = A.alloc([8, 1024], BF16)
        WUQ = [A.alloc([3, 256], BF16) for _ in range(2)]
        WUKV = [A.alloc([2, 256], BF16) for _ in range(2)]
        RT1 = A.alloc([512], F32, parts=64)
        RT2 = A.alloc([512], F32, parts=64)
        m1 = A.mark()
        WDKV = A.alloc([8, 768], BF16)
        wload(WDKV[:, :, 0:704], w_dkv)
        wload(WDKV[:, :, 704:736], w_dkv[:, 672:704])
        wload(WDKV[:, :, 736:768], w_dkv[:, 640:672])

        def loadh(h):
            sl = h % 2
            wload(WUQ[sl][:, :, 0:192], w_uq[:, h * 192:(h + 1) * 192])
            wload(WUQ[sl][:, :, 192:224], w_uq[:, h * 192 + 160:h * 192 + 192])
            wload(WUQ[sl][:, :, 224:256], w_uq[:, h * 192 + 128:h * 192 + 160])
            wload(WUKV[sl], w_ukv[:, h * 256:(h + 1) * 256])
        loadh(0)
        loadh(1)
        wload(WMO, w_mo)
        def rope(psA, psB, ts_, dst, scale):
            b.stt('dve', RT1, psA, float(scale), COS[:, ts_], ALU.mult, ALU.mult)
            b.stt('dve', RT2, psB, float(scale), SINS[:, ts_], ALU.mult, ALU.mult)
            b.tt('pool', dst, RT1, RT2, ALU.add)

        rmsnorm(HT[:, :, 1536:2048], 8, G_MIX[0], AN[:, :, 1536:2048], 512, 1024, rotAll, sq_eng='act')
        LAT = [A.alloc([5, 512], F32) for _ in range(2)]
        def lat_norms(tc):
            ts_ = slice(tc * 512, (tc + 1) * 512)
            lat = LAT[tc % 2]
            rmsnorm(lat[:, 0:3, :], 3, G_Q, CQ[:, :, ts_], 512, 384, rotAll, sq_eng='act')
            rmsnorm(lat[:, 3:5, :], 2, G_KV, CKV[:, :, ts_], 512, 256, rotAll, sq_eng='act')

        for tc in range(4):
            ts_ = slice(tc * 512, (tc + 1) * 512)
            lat = LAT[tc % 2]
            for m in range(5):
                bk = rotAll.next()
                for kc in range(8):
                    b.mm(bk, WDKV[:, kc, m * 128:(m + 1) * 128], AN[:, kc, ts_], start=(kc == 0), stop=(kc == 7))
                b.copy('act', lat[:, m, :], bk)
            bk1 = rotAll.next()
            bk2 = rotAll.next()
            for kc in range(8):
                b.mm(bk1[0:64, :], WDKV[:, kc, 640:704], AN[:, kc, ts_], start=(kc == 0), stop=(kc == 7))
            for kc in range(8):
                b.mm(bk2[0:64, :], WDKV[:, kc, 704:768], AN[:, kc, ts_], start=(kc == 0), stop=(kc == 7))
            rope(bk1[0:64, :], bk2[0:64, :], ts_, KPE[:, ts_], 1.0)
            if tc >= 1:
                lat_norms(tc - 1)
        lat_norms(3)
        A.release(m1)
        OT = AN
        QN = A.alloc([S], BF16)
        QR = A.alloc([S], BF16, parts=64)
        KN = A.alloc([S], BF16)
        V = A.alloc([16, 128], BF16)
        PT = [A.alloc([512], BF16) for _ in range(4)]
        RDEN = [A.alloc([512], F32) for _ in range(2)]
        rotZ = Rot([bank[0], bank[1], bank[2]])
        rotO = Rot([bank[3], bank[4], bank[5], bank[6]])
        rotA = Rot([bank[7], bank[0], bank[1], bank[2]])
        npt = 0
        for h in range(8):
            sl = h % 2
            for tc in range(4):
                ts_ = slice(tc * 512, (tc + 1) * 512)
                bk = rotA.next()
                for kc in range(3):
                    b.mm(bk, WUQ[sl][:, kc, 0:128], CQ[:, kc, ts_], start=(kc == 0), stop=(kc == 2))
                actmul(QN[:, ts_], bk, SC_MLA)
                bk1 = rotA.next()
                bk2 = rotA.next()
                for kc in range(3):
                    b.mm(bk1[0:64, :], WUQ[sl][:, kc, 128:192], CQ[:, kc, ts_], start=(kc == 0), stop=(kc == 2))
                for kc in range(3):
                    b.mm(bk2[0:64, :], WUQ[sl][:, kc, 192:256], CQ[:, kc, ts_], start=(kc == 0), stop=(kc == 2))
                rope(bk1[0:64, :], bk2[0:64, :], ts_, QR[:, ts_], SC_MLA)
                bk = rotA.next()
                for kc in range(2):
                    b.mm(bk, WUKV[sl][:, kc, 0:128], CKV[:, kc, ts_], start=(kc == 0), stop=(kc == 1))
                b.copy('act', KN[:, ts_], bk)
            for g in range(4):
                bk = rotA.next()
                for j in range(4):
                    tb = g * 4 + j
                    for kc in range(2):
                        b.mm(bk[:, j * 128:(j + 1) * 128], CKV[:, kc, tb * 128:(tb + 1) * 128],
                             WUKV[sl][:, kc, 128:256], start=(kc == 0), stop=(kc == 1))
                b.copy('dve', V[:, g * 4:(g + 1) * 4, :], bk.rearrange("p (j d) -> p j d", j=4))
            if h + 2 < 8:
                loadh(h + 2)
            pairs = []
            for qc in range(4):
                for kb in range(4 * qc + 4):
                    pairs.append((qc, kb))
            acc = {}
            pendq = []

            def pv(qc, kb, pt, c0):
                nkb = 4 * qc + 4
                OTa, DEN = acc[qc]
                b.add('pe', lambda e: e.matmul(OTa[:, c0:], V[:, kb, :], pt[:, c0:], start=(kb == 0),
                                               stop=(kb == nkb - 1), skip_group_check=True),
                      reads=[V[:, kb, :], pt[:, c0:]], writes=[OTa[:, c0:]])
                b.add('pe', lambda e: e.matmul(DEN[:, c0:], ONES, pt[:, c0:], start=(kb == 0),
                                               stop=(kb == nkb - 1), skip_group_check=True),
                      reads=[ONES, pt[:, c0:]], writes=[DEN[:, c0:]])
                if kb == nkb - 1:
                    q0 = qc * 512
                    rd = RDEN[qc % 2]
                    b.act(rd, DEN, AF.Ln)
                    b.act(rd, rd, AF.Exp, scale=-1.0)
                    b.tt('dve', OT[:, h, q0:q0 + 512], OTa, rd, ALU.mult)

            for (qc, kb) in pairs:
                if kb == 0:
                    acc[qc] = (rotO.next(), rotO.next())
                q0 = qc * 512
                j = kb - 4 * qc
                c0 = max(j, 0) * 128
                Z = rotZ.next()
                ks = slice(kb * 128, (kb + 1) * 128)
                b.mm(Z[:, c0:], KN[:, ks], QN[:, q0 + c0:q0 + 512], start=True, stop=False)
                b.mm(Z[:, c0:], KPE[:, ks], QR[:, q0 + c0:q0 + 512], start=False, stop=(j < 0))
                if j >= 0:
                    b.mm(Z[:, c0:c0 + 128], IDB, NMI, start=False, stop=True)
                pt = PT[npt % 4]
                npt += 1
                b.act(pt[:, c0:], Z[:, c0:], AF.Exp)
                pendq.append((qc, kb, pt, c0))
                if len(pendq) > 2:
                    pv(*pendq.pop(0))
            while pendq:
                pv(*pendq.pop(0))
        out_proj(WMO, 8, OT, rotAll, tail)
        A.release(m0)

    def xattn_phase(i, tail):
        m0 = A.mark()
        WQ = A.alloc([8, 512], BF16)
        WKV = A.alloc([8, 1024], BF16)
        MEMT = A.alloc([8, 256], F32)
        MN = A.alloc([8, 256], BF16)
        MEMS = A.alloc([2, 1024], F32)
        WXO = A.alloc([4, 1024], BF16)
        wload(WKV, xkv[i])
        wload(WQ, xq[i])
        wload(WXO, xo[i])
        KMT = A.alloc([4, 256], BF16)
        VM = A.alloc([2, 512], BF16)
        QX = A.alloc([4, S], BF16)
        OX = QX
        PT = [A.alloc([512], BF16) for _ in range(6)]
        RDEN = [A.alloc([512], F32) for _ in range(2)]
        b.dma('sp', MEMS, mem.rearrange("(t p) n -> p t n", p=128))
        for t in range(2):
            for half in range(2):
                bk = rotAll.next()
                for j in range(4):
                    c = half * 4 + j
                    b.tr(bk[:, j * 128:(j + 1) * 128], MEMS[:, t, c * 128:(c + 1) * 128], IDF)
                b.copy('act', MEMT[:, half * 4:(half + 1) * 4, t * 128:(t + 1) * 128],
                       bk.rearrange("p (c t) -> p c t", c=4))
        rmsnorm(MEMT, 8, G_MEM[i], MN, 256, 1024, rotAll, sq_eng='act')
        for hh in range(4):
            bk = rotAll.next()
            for kc in range(8):
                b.mm(bk[:, 0:256], WKV[:, kc, hh * 128:(hh + 1) * 128], MN[:, kc, :], start=(kc == 0), stop=(kc == 7))
            b.copy('act', KMT[:, hh, :], bk[:, 0:256])
        for mt in range(2):
            bk = rotAll.next()
            for kc in range(8):
                b.mm(bk, MN[:, kc, mt * 128:(mt + 1) * 128], WKV[:, kc, 512:1024], start=(kc == 0), stop=(kc == 7))
            b.copy('dve', VM[:, mt, :], bk)
        rotQ = Rot([bank[3]])

        def qproj(hh, tc):
            ts_ = slice(tc * 512, (tc + 1) * 512)
            bk = rotQ.next()
            for kc in range(8):
                b.mm(bk, WQ[:, kc, hh * 128:(hh + 1) * 128], AN[:, kc, ts_], start=(kc == 0), stop=(kc == 7))
            actmul(QX[:, hh, ts_], bk, SC_XA)

        for hh in range(4):
            qproj(hh, 0)
        rotZ = Rot([bank[0], bank[1], bank[2]])
        rotO = Rot([bank[4], bank[5], bank[6], bank[7]])
        npt = 0
        groups = [(hh, tc) for tc in range(4) for hh in range(4)]
        gst = {}

        def xa_front(g):
            nonlocal npt
            hh, tc = groups[g]
            ts_ = slice(tc * 512, (tc + 1) * 512)
            pts = []
            for mt in range(2):
                Z = rotZ.next()
                b.mm(Z, KMT[:, hh, mt * 128:(mt + 1) * 128], QX[:, hh, ts_], start=True, stop=True)
                pt = PT[npt % 6]
                npt += 1
                b.act(pt, Z, AF.Exp)
                pts.append(pt)
            gst[g] = pts

        def xa_back(g):
            hh, tc = groups[g]
            ts_ = slice(tc * 512, (tc + 1) * 512)
            pts = gst.pop(g)
            OTa = rotO.next()
            DEN = rotO.next()
            for mt in range(2):
                b.mm(DEN, ONES, pts[mt], start=(mt == 0), stop=(mt == 1))
            for mt in range(2):
                b.mm(OTa, VM[:, mt, hh * 128:(hh + 1) * 128], pts[mt], start=(mt == 0), stop=(mt == 1))
            rd = RDEN[g % 2]
            b.act(rd, DEN, AF.Ln)
            b.act(rd, rd, AF.Exp, scale=-1.0)
            b.tt('dve', OX[:, hh, ts_], OTa, rd, ALU.mult)

        for g in range(len(groups) + 1):
            if g < len(groups):
                xa_front(g)
                hh, tc = groups[g]
                if tc + 1 < 4:
                    if tc + 1 == 3 and hh == 0:
                        flush_pending()
                    qproj(hh, tc + 1)
            if g >= 1:
                xa_back(g - 1)
        pw_in = A.view(m0, [8, 1024], BF16)
        pw_out = A.view(m0 + 16384, [8, 1024], BF16)
        wload(pw_in, wi[i][:, 0:1024])
        wload(pw_out, wo[i][0:1024, :])
        mlp_pre[i] = True
        out_proj(WXO, 4, OX, rotAll, tail)
        A.release(m0)

    def mlp_phase(i, tail):
        m0 = A.mark()
        los = [None] * 4
        WIN = [None, None]
        WOUT = [None, None]
        WIN[0] = A.alloc([8, 1024], BF16)
        los[0] = A.last_lo
        WOUT[0] = A.alloc([8, 1024], BF16)
        los[2] = A.last_lo
        WIN[1] = A.alloc([8, 1024], BF16)
        los[1] = A.last_lo
        WOUT[1] = A.alloc([8, 1024], BF16)
        los[3] = A.last_lo
        fin_state['los'] = los
        R = [A.alloc([8, 512], BF16) for _ in range(2)]
        RL = [A.alloc([512], F32) for _ in range(3)]

        def loadg(fg):
            sl = fg % 2
            wload(WIN[sl], wi[i][:, fg * 1024:(fg + 1) * 1024])
            wload(WOUT[sl], wo[i][fg * 1024:(fg + 1) * 1024, :])
        if not mlp_pre.get(i):
            loadg(0)
        loadg(1)
        nrl = [0]
        blocks = [(fg, tc) for fg in range(4) for tc in range(4)]

        def u_block(k):
            fg, tc = blocks[k]
            sl = fg % 2
            ts_ = slice(tc * 512, (tc + 1) * 512)
            r = R[k % 2]
            for fc in range(8):
                bk = rotAll.next()
                for kc in range(8):
                    b.mm(bk, WIN[sl][:, kc, fc * 128:(fc + 1) * 128], AN[:, kc, ts_], start=(kc == 0), stop=(kc == 7))
                rl = RL[nrl[0] % 3]
                nrl[0] += 1
                b.act(rl, bk, AF.Relu)
                b.tt('pool', r[:, fc, :], rl, rl, ALU.mult)

        def y_block(k):
            fg, tc = blocks[k]
            sl = fg % 2
            ts_ = slice(tc * 512, (tc + 1) * 512)
            r = R[k % 2]
            for n in range(8):
                bk = rotAll.next()
                for fc in range(8):
                    b.mm(bk, WOUT[sl][:, fc, n * 128:(n + 1) * 128], r[:, fc, :], start=(fc == 0), stop=(fc == 7))
                b.tt('dve', HT[:, n, ts_], bk, HT[:, n, ts_], ALU.add)

        u_block(0)
        for k in range(16):
            fg, tc = blocks[k]
            if k + 1 < 16:
                u_block(k + 1)
            y_block(k)
            if fg == 0 and tc == 0:
                flush_pending()
            if fg == 3 and tc >= 1:
                tail(tc - 1)
            if tc == 3 and fg + 2 < 4:
                loadg(fg + 2)
        pending.append(lambda: tail(3))
        A.release(m0)

    def sb_phase(tail):
        m0 = A.mark()
        OT = A.alloc([8, S], BF16)
        WSO = A.alloc([8, 1024], BF16)
        WH = [A.alloc([8, 384], BF16) for _ in range(2)]
        QS = [A.alloc([S], BF16) for _ in range(2)]
        KS = [A.alloc([S], BF16) for _ in range(2)]
        VS = [A.alloc([16, 128], BF16) for _ in range(2)]
        E = [A.alloc([512], F32) for _ in range(2)]
        LB = [A.alloc([512], BF16) for _ in range(3)]
        ARG = [A.alloc([512], F32) for _ in range(2)]
        AT = [A.alloc([512], BF16) for _ in range(2)]
        RS = [A.alloc([512], F32) for _ in range(2)]
        rs_state = {}

        def loadh(h):
            sl = h % 2
            for part in range(3):
                wload(WH[sl][:, :, part * 128:(part + 1) * 128], w_qkv[:, part * 1024 + h * 128:part * 1024 + (h + 1) * 128])
        loadh(0)
        loadh(1)
        wload(WSO, w_so)
        rotZ = Rot([bank[0], bank[1], bank[2], bank[3]])
        rotC = Rot([bank[4], bank[5]])
        OTA = bank[6]
        PB = bank[7]

        def proj_steps(h):
            sl = h % 2
            steps = []
            for tc in range(4):
                ts_ = slice(tc * 512, (tc + 1) * 512)
                for which in range(2):
                    for half in range(2):
                        def f(tc=tc, ts_=ts_, which=which, half=half):
                            for kc in range(half * 4, half * 4 + 4):
                                b.mm(PB, WH[sl][:, kc, which * 128:(which + 1) * 128], AN[:, kc, ts_],
                                     start=(kc == 0), stop=(kc == 7))
                            if half == 1:
                                if which == 0:
                                    b.ts('dve', QS[sl][:, ts_], PB, float(SC_SB), None, ALU.mult)
                                else:
                                    b.copy('dve', KS[sl][:, ts_], PB)
                        steps.append(f)
            for g in range(4):
                for half in range(2):
                    def f(g=g, half=half):
                        for j in range(half * 2, half * 2 + 2):
                            tb = g * 4 + j
                            for kc in range(8):
                                b.mm(PB[:, j * 128:(j + 1) * 128], AN[:, kc, tb * 128:(tb + 1) * 128],
                                     WH[sl][:, kc, 256:384], start=(kc == 0), stop=(kc == 7))
                        if half == 1:
                            b.copy('dve', VS[sl][:, g * 4:(g + 1) * 4, :], PB.rearrange("p (j d) -> p j d", j=4))
                    steps.append(f)
            return steps

        st0 = proj_steps(0)
        qk = st0[0:16]
        vv = st0[16:24]
        for f in qk[0:12] + vv[0:6]:
            f()
        flush_pending()
        for f in qk[12:16] + vv[6:8]:
            f()
        loadh_pending = None
        for h in range(8):
            sl = h % 2
            Qh, Kh, Vh = QS[sl], KS[sl], VS[sl]
            nxt = proj_steps(h + 1) if h + 1 < 8 else []
            pairs = []
            for qc in range(4):
                for kb in reversed(range(4 * qc + 4)):
                    pairs.append((qc, kb))
            st = {}

            def stageZ(t):
                qc, kb = pairs[t]
                q0 = qc * 512
                j = kb - 4 * qc
                c0 = max(j, 0) * 128
                Z = rotZ.next()
                ks = slice(kb * 128, (kb + 1) * 128)
                b.mm(Z[:, c0:], Kh[:, ks], Qh[:, q0 + c0:q0 + 512], start=True, stop=(j < 0))
                if j >= 0:
                    b.mm(Z[:, c0:c0 + 128], IDB, NMS, start=False, stop=True)
                st[t] = dict(Z=Z, c0=c0, qc=qc, kb=kb)

            def stageE(t):
                s_ = st[t]
                c0, Z = s_['c0'], s_['Z']
                e = E[t % 2]
                b.act(e[:, c0:], Z[:, c0:], AF.Exp)
                s_['e'] = e

            def stageA2(t):
                s_ = st[t]
                c0 = s_['c0']
                lb = LB[t % 3]
                b.act(lb[:, c0:], s_['e'][:, c0:], AF.Ln, bias=1.0)
                s_['lb'] = lb

            def stageTRI(t):
                s_ = st[t]
                c0, Z, lb = s_['c0'], s_['Z'], s_['lb']
                CS = rotC.next()
                s_['CS'] = CS
                b.add('pe', lambda e: e.matmul(Z[:, c0:], NEGTRI, lb[:, c0:], start=False, stop=True,
                                               skip_group_check=True),
                      reads=[NEGTRI, lb[:, c0:]], writes=[Z[:, c0:]])
                b.mm(CS[:, c0:], NEGONES, lb[:, c0:], start=True, stop=True)

            def stageARG(t):
                s_ = st[t]
                c0, Z, CS, qc, kb = s_['c0'], s_['Z'], s_['CS'], s_['qc'], s_['kb']
                nkb = 4 * qc + 4
                arg = ARG[t % 2]
                if kb == nkb - 1:
                    rs_state['cur'] = 0
                    b.copy('dve', arg[:, c0:], Z[:, c0:])
                    b.copy('dve', RS[0][:, c0:], CS[:, c0:])
                    if c0 > 0:
                        b.memset('pool', RS[0][:, 0:c0], 0.0)
                else:
                    cur = rs_state['cur']
                    rs = RS[cur]
                    rn = RS[1 - cur]
                    b.tt('dve', arg[:, c0:], Z[:, c0:], rs[:, c0:], ALU.add)
                    b.tt('dve', rn[:, c0:], CS[:, c0:], rs[:, c0:], ALU.add)
                    if c0 > 0:
                        b.copy('pool', rn[:, 0:c0], rs[:, 0:c0])
                    rs_state['cur'] = 1 - cur
                s_['arg'] = arg

            def stageB2(t):
                s_ = st[t]
                c0 = s_['c0']
                at = AT[t % 2]
                b.act(at[:, c0:], s_['arg'][:, c0:], AF.Exp)
                s_['at'] = at

            def stageC(t):
                s_ = st[t]
                c0, qc, kb, at = s_['c0'], s_['qc'], s_['kb'], s_['at']
                nkb = 4 * qc + 4
                vblk = Vh[:, kb, :]
                b.add('pe', lambda e: e.matmul(OTA[:, c0:], vblk, at[:, c0:], start=(kb == nkb - 1),
                                               stop=(kb == 0), skip_group_check=True),
                      reads=[vblk, at[:, c0:]], writes=[OTA[:, c0:]])
                if kb == 0:
                    q0 = qc * 512
                    b.copy('dve', OT[:, h, q0:q0 + 512], OTA)
                del st[t]

            NPR = len(pairs)
            for t in range(NPR + 6):
                if t < NPR:
                    stageZ(t)
                if 0 <= t - 1 < NPR:
                    stageE(t - 1)
                if 0 <= t - 4 < NPR:
                    stageB2(t - 4)
                if 0 <= t - 1 < NPR:
                    stageA2(t - 1)
                if nxt and t >= 1 and (t * 24) // 38 > ((t - 1) * 24) // 38:
                    nxt.pop(0)()
                    if not nxt and h + 2 < 8:
                        loadh(h + 2)
                if 0 <= t - 2 < NPR:
                    stageTRI(t - 2)
                if 0 <= t - 3 < NPR:
                    stageARG(t - 3)
                if 0 <= t - 5 < NPR:
                    stageC(t - 5)
            while nxt:
                nxt.pop(0)()
        out_proj(WSO, 8, OT, rotAll, tail)
        A.release(m0)

    def final_tail(tc):
        if 'OUTF' not in fin_state:
            los = fin_state['los']
            fin_state['OUTF'] = [A.view(los[0], [8, 512], F32)]
            fin_state['OST'] = [A.view(los[2], [1024], F32), A.view(los[2] + 4096, [1024], F32)]
        OUTF, OST = fin_state['OUTF'], fin_state['OST']
        ts_ = slice(tc * 512, (tc + 1) * 512)
        of = OUTF[0]
        rmsnorm(HT[:, :, ts_], 8, G_FINAL, of, 512, 1024, rotAll, sq_eng='act')
        for tbl in range(4):
            tb = tc * 4 + tbl
            ost = OST[tb % 2]
            for half in range(2):
                bk = rotAll.next()
                for j in range(4):
                    c = half * 4 + j
                    b.tr(bk[:, j * 128:(j + 1) * 128], of[:, c, tbl * 128:(tbl + 1) * 128], IDF)
                b.copy('act' if half == 0 else 'dve', ost[:, half * 512:(half + 1) * 512], bk)
            b.dma('sp', out[tb * 128:(tb + 1) * 128, :], ost, is_output=True)

    def no_tail(tc):
        pass

    def tail_for(k):
        if stage < 99 and k >= stage:
            return no_tail
        return [None, norm_tail(G_CROSS[0]), norm_tail(G_MLP[0]), norm_tail(G_MIX[1]),
                norm_tail(G_CROSS[1]), norm_tail(G_MLP[1]), final_tail][k]

    if stage >= 1:
        mla_phase(tail_for(1), COS, SINS, m_mla)
    if stage >= 2:
        xattn_phase(0, tail_for(2))
    if stage >= 3:
        mlp_phase(0, tail_for(3))
    if stage >= 4:
        sb_phase(tail_for(4))
    if stage >= 5:
        xattn_phase(1, tail_for(5))
    if stage >= 6:
        mlp_phase(1, tail_for(6))
    flush_pending()
    if stage < 99:
        b.dma('sp', dbg.rearrange("p (c t) -> p c t", c=8), HT, is_output=True)
    info = b.emit()
    return nc, info, A.peak


def make_consts():
    j = np.arange(128)[:, None]
    k = np.arange(128)[None, :]
    idf = np.eye(128, dtype=np.float32)
    ones = np.ones((128, 128), np.float32)
    negtri = np.where(j >= k, -1.0, 0.0).astype(np.float32)
    nms = np.where(j >= k, NEGV, 0.0).astype(np.float32)
    nmi = np.where(j > k, NEGV, 0.0).astype(np.float32)
    return np.ascontiguousarray(np.concatenate([idf, idf, ones, -ones, negtri, nms, nmi], axis=1))


def colmaj(g):
    g = np.asarray(g, np.float32)
    return g.reshape(-1, 128).T


def make_gv(inp):
    gvv = np.zeros((128, NG), np.float32)
    for i in range(2):
        gvv[:, G_MIX[i]:G_MIX[i] + 8] = colmaj(inp["norm_mix"][i])
        gvv[:, G_CROSS[i]:G_CROSS[i] + 8] = colmaj(inp["norm_cross"][i])
        gvv[:, G_MEM[i]:G_MEM[i] + 8] = colmaj(inp["norm_mem"][i])
        gvv[:, G_MLP[i]:G_MLP[i] + 8] = colmaj(inp["norm_mlp"][i])
    gvv[:, G_FINAL:G_FINAL + 8] = colmaj(inp["norm_final"])
    gvv[:, G_Q:G_Q + 3] = colmaj(inp["mla_g_q"][0])
    gvv[:, G_KV:G_KV + 2] = colmaj(inp["mla_g_kv"][0])
    invf = (np.float32(10000.0) ** (-np.arange(0, 64, 2, dtype=np.float32) / np.float32(64))).astype(np.float32)
    gvv[0:32, G_INVF] = invf
    gvv[32:64, G_INVF] = invf
    return gvv


def make_in_maps(inp, cores):
    f = lambda a: np.ascontiguousarray(np.asarray(a, dtype=np.float32))
    shared = dict(
        gv=make_gv(inp), cst=make_consts(),
        w_dkv=f(inp["mla_w_dkv"][0]), w_uq=f(inp["mla_w_uq"][0]), w_ukv=f(inp["mla_w_ukv"][0]),
        w_mo=f(inp["mla_w_o"][0]), w_qkv=f(inp["sb_w_qkv"][0]), w_so=f(inp["sb_w_o"][0]),
    )
    for i in range(2):
        shared[f"xq{i}"] = f(inp["xa_w_q"][i])
        shared[f"xkv{i}"] = f(inp["xa_w_kv"][i])
        shared[f"xo{i}"] = f(inp["xa_w_o"][i])
        shared[f"wi{i}"] = f(inp["mlp_w_in"][i])
        shared[f"wo{i}"] = f(inp["mlp_w_out"][i])
    maps = []
    for c in cores:
        m = dict(shared)
        m["x"] = f(inp["x"][c])
        m["mem"] = f(inp["mem"][c])
        m["pos"] = np.ascontiguousarray(np.asarray(inp["positions"][c], dtype=np.int32).reshape(1, S))
        maps.append(m)
    return maps


def kernel(**inputs):
    nc, info, peak = build_program(99)
    maps = make_in_maps(inputs, list(range(8)))
    res = run_bass_kernel_spmd(nc, maps, core_ids=list(range(8)))
    return np.stack([np.asarray(r["out"], dtype=np.float32) for r in res.results], axis=0)
```
